# Optimizing a Trainium2 kernel written in Bass

```python
import jax
import jax.numpy as jnp
from jax import lax
import numpy as np

D_MODEL = 1024
BATCH = 4
SEQ = 8192
DEPTH = 1

GRID_W = 64
CTX_LEN = 256
N_HEADS = 8
HEAD_DIM = 64
ATTN_WIDTH = N_HEADS * HEAD_DIM
POOL_WINDOWS = (2, 4, 8, 16)
N_POOL_GROUPS = len(POOL_WINDOWS)
POOL_WIDTH = 512
POOL_GROUP = POOL_WIDTH // N_POOL_GROUPS
GATE_WIDTH = 2 * D_MODEL
IN_COLS = 3 * ATTN_WIDTH + POOL_WIDTH + GATE_WIDTH
WIN_ROWS = 8
WIN_COLS = 16
Q_COL_BLOCK = 16
K_COL_BLOCK = Q_COL_BLOCK + WIN_COLS
ROPE_FREQS = HEAD_DIM // 4
ROPE_THETA = 10000.0
D_FF = -(-8 * D_MODEL // (3 * 256)) * 256
N_MOD = 6
EPS = 1e-6
NEG_INF = -1e30

kernel_name = 'hybrid_natten_multipool_block'


def rms_norm(x, g):
    xf = x.astype(jnp.float32)
    y = xf * lax.rsqrt(jnp.mean(xf * xf, axis=-1, keepdims=True) + EPS)
    return (y * g.astype(jnp.float32)).astype(x.dtype)


def modulate(x, shift, scale):
    return x * (1.0 + scale) + shift


def heads(t):
    return t.reshape(*t.shape[:-1], N_HEADS, HEAD_DIM)


def split_projection(proj):
    a = ATTN_WIDTH
    q = proj[..., :a]
    k = proj[..., a:2 * a]
    v = proj[..., 2 * a:3 * a]
    p = proj[..., 3 * a:3 * a + POOL_WIDTH]
    gates = proj[..., 3 * a + POOL_WIDTH:]
    return q, k, v, p, gates


def axial_rope_tables(n_tok):
    t = jnp.arange(n_tok)
    pos = jnp.stack([t // GRID_W, t % GRID_W], axis=-1).astype(jnp.float32)
    inv = ROPE_THETA ** (-jnp.arange(ROPE_FREQS, dtype=jnp.float32) / ROPE_FREQS)
    ang = pos[:, :, None] * inv
    return jnp.cos(ang), jnp.sin(ang)


def apply_rope(x, cos, sin):
    b, s, h, dh = x.shape
    xf = x.astype(jnp.float32).reshape(b, s, h, 2, 2, ROPE_FREQS)
    x1, x2 = xf[..., 0, :], xf[..., 1, :]
    cs, sn = cos[:, None], sin[:, None]
    out = jnp.stack([x1 * cs - x2 * sn, x2 * cs + x1 * sn], axis=-2)
    return out.reshape(b, s, h, dh).astype(x.dtype)


def neighbourhood_attention(q, k, v, k_ctx, v_ctx, rpb):
    b, s, h, dh = q.shape
    rows = s // GRID_W
    kr = min(WIN_ROWS, rows)
    n_cb = GRID_W // Q_COL_BLOCK
    scale = dh ** -0.5
    qg = q.reshape(b, rows, n_cb, Q_COL_BLOCK, h, dh)
    kg = k.reshape(b, rows, GRID_W, h, dh)
    vg = v.reshape(b, rows, GRID_W, h, dh)
    cb = jnp.arange(n_cb)
    q_cols = cb[:, None] * Q_COL_BLOCK + jnp.arange(Q_COL_BLOCK)[None, :]
    q_col0 = jnp.clip(q_cols - WIN_COLS // 2, 0, GRID_W - WIN_COLS)
    k_col0 = jnp.clip(cb * Q_COL_BLOCK - WIN_COLS // 2, 0, GRID_W - K_COL_BLOCK)
    k_cols = k_col0[:, None] + jnp.arange(K_COL_BLOCK)[None, :]
    kc = k_cols[:, None, :]
    col_valid = (kc >= q_col0[..., None]) & (kc < q_col0[..., None] + WIN_COLS)
    col_idx = jnp.clip(kc - q_cols[..., None] + WIN_COLS - 1, 0, 2 * WIN_COLS - 2)
    mask = col_valid[:, :, None, :]
    n_win = kr * K_COL_BLOCK

    def row_block(r):
        r0 = jnp.clip(r - kr // 2, 0, rows - kr)
        q_r = lax.dynamic_index_in_dim(qg, r, axis=1, keepdims=False)
        k_band = lax.dynamic_slice_in_dim(kg, r0, kr, axis=1)[:, :, k_cols]
        v_band = lax.dynamic_slice_in_dim(vg, r0, kr, axis=1)[:, :, k_cols]
        row_idx = r0 + jnp.arange(kr) - r + WIN_ROWS - 1
        bias = rpb[:, row_idx[None, None, :, None], col_idx[:, :, None, :]]
        s_win = jnp.einsum('bnqhd,brnkhd->bhnqrk', q_r, k_band,
                           preferred_element_type=jnp.float32) * scale
        s_win = jnp.where(mask, s_win + bias.astype(jnp.float32), NEG_INF)
        s_ctx = jnp.einsum('bnqhd,blhd->bhnql', q_r, k_ctx,
                           preferred_element_type=jnp.float32) * scale
        scores = jnp.concatenate([s_win.reshape(b, h, n_cb, Q_COL_BLOCK, n_win), s_ctx], axis=-1)
        p = jax.nn.softmax(scores, axis=-1).astype(v.dtype)
        p_win = p[..., :n_win].reshape(b, h, n_cb, Q_COL_BLOCK, kr, K_COL_BLOCK)
        out = (jnp.einsum('bhnqrk,brnkhd->bnqhd', p_win, v_band)
               + jnp.einsum('bhnql,blhd->bnqhd', p[..., n_win:], v_ctx))
        return out.reshape(b, GRID_W, h * dh)

    out = lax.map(row_block, jnp.arange(rows))
    return jnp.moveaxis(out, 0, 1).reshape(b, s, h * dh)


def context_attention(q, k, v):
    b, l, h, dh = q.shape
    scores = jnp.einsum('blhd,bmhd->bhlm', q, k, preferred_element_type=jnp.float32) * dh ** -0.5
    p = jax.nn.softmax(scores, axis=-1).astype(v.dtype)
    return jnp.einsum('bhlm,bmhd->blhd', p, v).reshape(b, l, h * dh)


def multiscale_pool(p, pool_w, pool_scale):
    b, n, _ = p.shape
    pf = p.astype(jnp.float32).reshape(b, n, N_POOL_GROUPS, POOL_GROUP)
    csum = jnp.concatenate([jnp.zeros_like(pf[:, :1]), jnp.cumsum(pf, axis=1)], axis=1)
    t = jnp.arange(n)
    groups = []
    for g, w in enumerate(POOL_WINDOWS):
        lo = jnp.clip(t - w // 2, 0, n)
        hi = jnp.clip(t + w - w // 2, 0, n)
        cnt = (hi - lo).astype(jnp.float32)[None, :, None]
        mean = (csum[:, hi, g] - csum[:, lo, g]) / cnt
        groups.append(mean - pf[:, :, g])
    pooled = jnp.stack(groups, axis=2).astype(p.dtype)
    mixed = jnp.einsum('bngc,gcd->bngd', pooled, pool_w)
    return mixed.reshape(b, n, POOL_WIDTH) * pool_scale


def merge_branches(attn, pool, gates, b_gate, w_proj_a, w_proj_b, w_out):
    g = jax.nn.sigmoid(gates + b_gate)
    g_a, g_b = jnp.split(g, 2, axis=-1)
    return (g_a * (attn @ w_proj_a) + g_b * (pool @ w_proj_b)) @ w_out


def swiglu(h, w_up, w_down):
    gate, up = jnp.split(h @ w_up, 2, axis=-1)
    return (jax.nn.silu(gate) * up) @ w_down


def setup_inputs(seed: int = 0) -> dict:
    key = jax.random.key(seed)
    ks = jax.random.split(key, 20)

    def nrm(k, shape, s):
        return jax.random.normal(k, shape, jnp.float32) * s

    return {
        'x': nrm(ks[0], (BATCH, SEQ, D_MODEL), 1.0),
        'c': nrm(ks[1], (BATCH, D_MODEL), 1.0),
        'ctx': nrm(ks[2], (BATCH, CTX_LEN, D_MODEL), 1.0),
        'c_ctx': nrm(ks[3], (D_MODEL,), 1.0),
        'w_ada': nrm(ks[4], (DEPTH, D_MODEL, N_MOD * D_MODEL), 0.5 * D_MODEL ** -0.5),
        'b_ada': nrm(ks[5], (DEPTH, N_MOD * D_MODEL), 0.02),
        'g_pre_mix': 1.0 + nrm(ks[6], (DEPTH, D_MODEL), 0.05),
        'g_post_mix': 1.0 + nrm(ks[7], (DEPTH, D_MODEL), 0.05),
        'g_pre_ffn': 1.0 + nrm(ks[8], (DEPTH, D_MODEL), 0.05),
        'g_post_ffn': 1.0 + nrm(ks[9], (DEPTH, D_MODEL), 0.05),
        'w_in': nrm(ks[10], (DEPTH, D_MODEL, IN_COLS), D_MODEL ** -0.5),
        'b_gate': nrm(ks[11], (DEPTH, GATE_WIDTH), 0.02),
        'rpb': nrm(ks[12], (DEPTH, N_HEADS, 2 * WIN_ROWS - 1, 2 * WIN_COLS - 1), 0.02),
        'pool_w': nrm(ks[13], (DEPTH, N_POOL_GROUPS, POOL_GROUP, POOL_GROUP), POOL_GROUP ** -0.5),
        'pool_scale': 1.0 + nrm(ks[14], (DEPTH, POOL_WIDTH), 0.1),
        'w_proj_a': nrm(ks[15], (DEPTH, ATTN_WIDTH, D_MODEL), ATTN_WIDTH ** -0.5),
        'w_proj_b': nrm(ks[16], (DEPTH, POOL_WIDTH, D_MODEL), POOL_WIDTH ** -0.5),
        'w_out': nrm(ks[17], (DEPTH, D_MODEL, D_MODEL), D_MODEL ** -0.5),
        'w_up': nrm(ks[18], (DEPTH, D_MODEL, 2 * D_FF), D_MODEL ** -0.5),
        'w_down': nrm(ks[19], (DEPTH, D_FF, D_MODEL), D_FF ** -0.5),
    }


def reference(x, c, ctx, c_ctx, w_ada, b_ada, g_pre_mix, g_post_mix, g_pre_ffn, g_post_ffn,
              w_in, b_gate, rpb, pool_w, pool_scale, w_proj_a, w_proj_b, w_out, w_up, w_down):
    cos, sin = axial_rope_tables(x.shape[1])
    for layer in range(DEPTH):
        update_ctx = layer < DEPTH - 1
        mod = jax.nn.silu(c) @ w_ada[layer] + b_ada[layer]
        sh1, sc1, gt1, sh2, sc2, gt2 = jnp.split(mod[:, None, :], N_MOD, axis=-1)
        n_ctx_mod = N_MOD if update_ctx else 2
        mod_ctx = (jax.nn.silu(c_ctx) @ w_ada[layer][:, :n_ctx_mod * D_MODEL]
                   + b_ada[layer][:n_ctx_mod * D_MODEL])
        mod_ctx = jnp.split(mod_ctx, n_ctx_mod)

        h = modulate(rms_norm(x, g_pre_mix[layer]), sh1, sc1)
        h_ctx = modulate(rms_norm(ctx, g_pre_mix[layer]), mod_ctx[0], mod_ctx[1])
        q, k, v, p, gates = split_projection(h @ w_in[layer])
        if update_ctx:
            q_c, k_c, v_c, p_c, gates_c = split_projection(h_ctx @ w_in[layer])
        else:
            k_c, v_c = jnp.split(h_ctx @ w_in[layer][:, ATTN_WIDTH:3 * ATTN_WIDTH], 2, axis=-1)
        k_c, v_c = heads(k_c), heads(v_c)
        q = apply_rope(heads(q), cos, sin)
        k = apply_rope(heads(k), cos, sin)
        attn = neighbourhood_attention(q, k, heads(v), k_c, v_c, rpb[layer])
        pool = multiscale_pool(p, pool_w[layer], pool_scale[layer])
        y = merge_branches(attn, pool, gates, b_gate[layer], w_proj_a[layer], w_proj_b[layer], w_out[layer])
        x = x + gt1 * rms_norm(y, g_post_mix[layer])
        if update_ctx:
            attn_c = context_attention(heads(q_c), k_c, v_c)
            pool_c = multiscale_pool(p_c, pool_w[layer], pool_scale[layer])
            y_c = merge_branches(attn_c, pool_c, gates_c, b_gate[layer], w_proj_a[layer],
                                 w_proj_b[layer], w_out[layer])
            ctx = ctx + mod_ctx[2] * rms_norm(y_c, g_post_mix[layer])

        h = modulate(rms_norm(x, g_pre_ffn[layer]), sh2, sc2)
        x = x + gt2 * rms_norm(swiglu(h, w_up[layer], w_down[layer]), g_post_ffn[layer])
        if update_ctx:
            h_c = modulate(rms_norm(ctx, g_pre_ffn[layer]), mod_ctx[3], mod_ctx[4])
            ctx = ctx + mod_ctx[5] * rms_norm(swiglu(h_c, w_up[layer], w_down[layer]), g_post_ffn[layer])
    return x
```

```python
import contextlib
import bisect
import numpy as np
import ml_dtypes
import concourse.bass as bass
import concourse.mybir as mybir
from concourse.bass_utils import run_bass_kernel_spmd

F32 = mybir.dt.float32
BF16 = mybir.dt.bfloat16
AF = mybir.ActivationFunctionType
ALU = mybir.AluOpType

D = 1024
SEQ = 8192
GW = 64
T = 256
NB = 16
NE = NB + 2
NCORE = 8
DFF = 2816
NJ = DFF // 128
EPS = 1e-6
KCOL0 = [0, 8, 24, 32]
SEM_LIM = 30000


class Sched:
    def __init__(self, nc, es):
        self.nc = nc
        self.es = es
        self.eng = {"pe": nc.tensor, "act": nc.scalar, "dve": nc.vector, "pool": nc.gpsimd, "sp": nc.sync}
        self.n = 0
        self.floor = 0
        self.info = {}
        self.lastw = {}
        self.readers = {}
        self.cnt = {e: 0 for e in self.eng}
        self.esems = {e: [] for e in self.eng}
        self.sig_i = {e: [] for e in self.eng}
        self.sig_v = {e: [] for e in self.eng}
        self.seen = {e: {} for e in self.eng}
        self.slots = {}
        self.bar_deps = {e: [] for e in self.eng}
        self.last_on = {}

    def _newsem(self, name):
        return self.es.enter_context(self.nc.semaphore(name))

    def _sig_of(self, E, d):
        k = bisect.bisect_left(self.sig_i[E], d)
        assert k < len(self.sig_i[E]), "dependency on %s op %d has no signalling successor yet" % (E, d)
        n = self.sig_v[E][k]
        return self.esems[E][n // SEM_LIM], n % SEM_LIM + 1

    def op(self, eng, fn, r=(), w=(), slot=None, sig=True):
        idx = self.n
        self.n += 1
        hb_ = [x for x in list(r) + list(w) if isinstance(x, tuple) and len(x) == 2 and x[0] == "H"]
        if hb_:
            r = [x for x in r if x not in hb_]
            w = [x for x in w if x not in hb_] + list({("B", x[1] // 2) for x in hb_})
        deps = {}
        for res in r:
            d = self.lastw.get(res)
            if d is not None and d >= self.floor:
                deps[d] = True
        for res in w:
            live = [d for d in self.readers.get(res, {}).values() if d >= self.floor]
            if live:
                for d in live:
                    deps.setdefault(d, False)
            else:
                d = self.lastw.get(res)
                if d is not None and d >= self.floor:
                    deps.setdefault(d, False)
        for d in self.bar_deps[eng]:
            deps[d] = True
        self.bar_deps[eng] = []
        engine = self.eng[eng]
        waits = {}
        for d, raw in deps.items():
            dE, dslot = self.info[d]
            if dslot is not None:
                s = self.slots[dslot]
                sem, val = s["sem"], s["cum"]
            else:
                if dE == eng and (eng == "pe" or not raw):
                    continue
                sem, val = self._sig_of(dE, d)
            key = id(sem)
            if key not in waits or waits[key][1] < val:
                waits[key] = (sem, val)
        for key, (sem, val) in waits.items():
            if self.seen[eng].get(key, 0) >= val:
                continue
            engine.wait_ge(sem, val)
            self.seen[eng][key] = val
        ins = fn()
        if slot is not None:
            s = self.slots.get(slot)
            if s is None:
                s = dict(sem=self._newsem("d%d" % len(self.slots)), cum=0)
                self.slots[slot] = s
            s["cum"] += 16
            ins.then_inc(s["sem"], 16)
        elif sig:
            n = self.cnt[eng]
            self.cnt[eng] += 1
            while len(self.esems[eng]) <= n // SEM_LIM:
                self.esems[eng].append(self._newsem("e_%s%d" % (eng, len(self.esems[eng]))))
            ins.then_inc(self.esems[eng][n // SEM_LIM], 1)
            self.sig_i[eng].append(idx)
            self.sig_v[eng].append(n)
        self.info[idx] = (eng, slot)
        for res in r:
            rd = self.readers.setdefault(res, {})
            rd[eng if slot is None else ("dma", idx)] = idx
        for res in w:
            self.lastw[res] = idx
            self.readers[res] = {}
        if slot is None:
            self.last_on[eng] = idx
        return idx

    def barrier(self):
        deps = []
        for e, idx in self.last_on.items():
            if idx >= self.floor:
                deps.append(idx)
        for i in range(self.floor, self.n):
            if self.info[i][1] is not None:
                deps.append(i)
        self.floor = self.n
        for e in self.eng:
            self.bar_deps[e] = list(deps)

    def final_wait(self, eng="sp"):
        engine = self.eng[eng]
        for s in self.slots.values():
            if s["cum"] > 0:
                engine.wait_ge(s["sem"], s["cum"])


def _chunkT(v, n):
    return np.ascontiguousarray(v.reshape(n, 128).T)


def _swap_halves(w):
    return w.reshape(w.shape[0], 8, 2, 2, 16)[:, :, :, ::-1, :].reshape(w.shape[0], 512)


def _rope_tables(tok0):
    tok = tok0 + np.arange(NE * T)
    tok = np.clip(tok, 0, SEQ - 1)
    row = (tok // GW).astype(np.float32)
    col = (tok % GW).astype(np.float32)
    p = np.arange(128)
    d = p % 64
    axis = d // 32
    hf = (d // 16) % 2
    f = d % 16
    inv = (np.float32(10000.0) ** (-(f.astype(np.float32)) / np.float32(16))).astype(np.float32)
    pos = np.where(axis[:, None] == 0, row[None, :], col[None, :]).astype(np.float32)
    ang = (pos * inv[:, None]).astype(np.float32)
    C = np.cos(ang).astype(np.float32)
    S = (np.sin(ang) * np.where(hf == 1, 1.0, -1.0)[:, None]).astype(np.float32)
    C = np.ascontiguousarray(C.reshape(128, NE, T).transpose(1, 0, 2))
    S = np.ascontiguousarray(S.reshape(128, NE, T).transpose(1, 0, 2))
    return C, S


def _mask_tables(rpb, half):
    out = np.empty((3, 8, 128, 768), np.float32)
    n = np.arange(4)[:, None, None, None, None, None]
    jc = np.arange(3)[None, :, None, None, None, None]
    i = np.arange(4)[None, None, :, None, None, None]
    jq = np.arange(16)[None, None, None, :, None, None]
    krl = np.arange(4)[None, None, None, None, :, None]
    kcl = np.arange(32)[None, None, None, None, None, :]
    k0 = np.array(KCOL0)[n]
    for cls in range(3):
        a = [16 * half, 5, 16 * half + 15][cls]
        r = 4 * a + i
        qc = 16 * n + jq
        kr = 4 * (a - 1 + jc) + krl
        kc = k0 + kcl
        r0 = np.clip(r - 4, 0, 120)
        qc0 = np.clip(qc - 8, 0, 48)
        valid = (kr >= 0) & (kr < 128) & (kr >= r0) & (kr < r0 + 8) & (kc >= qc0) & (kc < qc0 + 16)
        ri = np.clip(kr - r + 7, 0, 14)
        ci = np.clip(kc - qc + 15, 0, 30)
        shp = np.broadcast_shapes(valid.shape, ri.shape, ci.shape)
        ri = np.broadcast_to(ri, shp)
        ci = np.broadcast_to(ci, shp)
        valid = np.broadcast_to(valid, shp)
        for h in range(8):
            b = rpb[h][ri, ci]
            b = np.where(valid, b, np.float32(-30000.0)).astype(np.float32)
            out[cls, h] = b.transpose(4, 5, 0, 1, 2, 3).reshape(128, 768)
    return out


def _band_tables(half):
    out = np.zeros((3, 128, 3, 4, 128), np.float32)
    for cls in range(3):
        gt = [32 * half, 5, 32 * half + 31][cls]
        to = gt * 128 + np.arange(128)
        for g, w in enumerate((2, 4, 8, 16)):
            lo = np.clip(to - w // 2, 0, SEQ)
            hi = np.clip(to + w - w // 2, 0, SEQ)
            cnt = (hi - lo).astype(np.float32)
            for rel in range(3):
                ts = (gt - 1 + rel) * 128 + np.arange(128)
                inwin = (ts[:, None] >= lo[None, :]) & (ts[:, None] < hi[None, :])
                wgt = np.where(inwin, 1.0 / cnt[None, :], 0.0) - (ts[:, None] == to[None, :])
                out[cls, :, rel, g, :] = wgt
    return out.reshape(3, 128, 1536)


def _prep(inp):
    f = lambda k: np.asarray(inp[k], dtype=np.float32)
    x, c, ctx, c_ctx = f("x"), f("c"), f("ctx"), f("c_ctx")
    shared = {}
    w_ada = f("w_ada")[0]
    shared["w_ada"] = np.ascontiguousarray(w_ada.reshape(8, 128, 12, 512).transpose(2, 0, 1, 3))
    shared["id2"] = np.eye(2, dtype=np.float32)
    shared["b_ada"] = _chunkT(f("b_ada")[0], 48)
    shared["gvec"] = np.ascontiguousarray(np.stack(
        [_chunkT(f(k)[0], 8) for k in ("g_pre_mix", "g_post_mix", "g_pre_ffn", "g_post_ffn")], axis=1))
    w_in = f("w_in")[0]
    q, k, v, p, g = w_in[:, :512], w_in[:, 512:1024], w_in[:, 1024:1536], w_in[:, 1536:2048], w_in[:, 2048:]
    shared["w_in"] = np.ascontiguousarray(w_in.reshape(8, 128, 8, 512).transpose(2, 1, 0, 3))
    pm = np.zeros((128, 128), np.float32)
    pm[np.arange(128) ^ 16, np.arange(128)] = 1.0
    shared["perm"] = pm
    shared["b_gate"] = _chunkT(f("b_gate")[0], 16)
    shared["pool_w"] = np.ascontiguousarray(f("pool_w")[0].transpose(1, 0, 2))
    shared["pool_scale"] = _chunkT(f("pool_scale")[0], 4)
    shared["w_pa"] = np.ascontiguousarray(f("w_proj_a")[0].reshape(4, 128, 1024).transpose(1, 0, 2))
    shared["w_pb"] = np.ascontiguousarray(f("w_proj_b")[0].reshape(4, 128, 1024).transpose(1, 0, 2))
    shared["w_out"] = np.ascontiguousarray(f("w_out")[0].reshape(8, 128, 1024).transpose(1, 0, 2))
    shared["w_up"] = np.ascontiguousarray(f("w_up")[0].reshape(8, 128, 11, 512).transpose(2, 1, 0, 3))
    shared["w_dn"] = np.ascontiguousarray(f("w_down")[0].reshape(NJ, 128, 1024).transpose(1, 0, 2))
    rpb = f("rpb")[0]
    per_half = []
    for half in range(2):
        C, S = _rope_tables(half * 4096 - T)
        per_half.append(dict(ropeC=C, ropeS=S, mask_add=_mask_tables(rpb, half), band=_band_tables(half)))
    maps = []
    for cid in range(NCORE):
        b, half = cid // 2, cid % 2
        tok0 = half * 4096 - T
        xe = np.zeros((NE * T, D), np.float32)
        lo, hi = max(tok0, 0), min(tok0 + NE * T, SEQ)
        xe[lo - tok0:hi - tok0] = x[b, lo:hi]
        m = dict(shared)
        m.update(per_half[half])
        m["xT"] = np.ascontiguousarray(xe.reshape(NE, T, 8, 128).transpose(0, 3, 2, 1))
        m["ctxT"] = np.ascontiguousarray(ctx[b].reshape(T, 8, 128).transpose(2, 1, 0))
        m["cvec"] = np.ascontiguousarray(np.stack([_chunkT(c[b], 8), _chunkT(c_ctx, 8)], axis=2))
        maps.append(m)
    return maps


class HBRot:
    def __init__(self, banks_):
        self.ids = list(banks_)
        self.k = 0

    def bank(self):
        b = self.ids[self.k % len(self.ids)]
        self.k += 1
        return 2 * b


def build(debug=False, nphase=3):
    nc = bass.Bass("TRN2", target_bir_lowering=False)
    es = contextlib.ExitStack()
    S = Sched(nc, es)
    pe, act, dve, pool, sp = nc.tensor, nc.scalar, nc.vector, nc.gpsimd, nc.sync

    def din(name, shape, dt=F32):
        return nc.dram_tensor(name, list(shape), dt, kind="ExternalInput").ap()

    def dscr(name, shape, dt):
        kind = "ExternalOutput" if debug else "Internal"
        return nc.dram_tensor(name, list(shape), dt, kind=kind).ap()

    xT = din("xT", [NE, 128, 8, T])
    ctxT = din("ctxT", [128, 8, T])
    cvec = din("cvec", [128, 8, 2])
    w_ada = din("w_ada", [12, 8, 128, 512])
    id2 = din("id2", [2, 2])
    b_ada = din("b_ada", [128, 48])
    gvec = din("gvec", [128, 4, 8])
    w_in = din("w_in", [8, 128, 8, 512])
    perm = din("perm", [128, 128])
    b_gate = din("b_gate", [128, 16])
    ropeC = din("ropeC", [NE, 128, T])
    ropeS = din("ropeS", [NE, 128, T])
    mask_add = din("mask_add", [3, 8, 128, 768])
    band = din("band", [3, 128, 1536])
    pool_w = din("pool_w", [128, 4, 128])
    pool_scale = din("pool_scale", [128, 4])
    w_pa = din("w_pa", [128, 4, 1024])
    w_pb = din("w_pb", [128, 4, 1024])
    w_out = din("w_out", [128, 8, 1024])
    w_up = din("w_up", [11, 128, 8, 512])
    w_dn = din("w_dn", [128, NJ, 1024])
    outT = nc.dram_tensor("outT", [NB, 128, 8, T], F32, kind="ExternalOutput").ap()

    s_q = dscr("s_q", [NB, 128, 4, T], BF16)
    s_kc = dscr("s_kc", [NE, 128, 4, 4, 128], BF16)
    s_v = dscr("s_v", [NE * T, 768], BF16)
    s_p = dscr("s_p", [NE, 2, 128, 512], BF16)
    s_g = dscr("s_g", [NB, 128, 16, T], BF16)
    s_x1 = dscr("s_x1", [NB, 128, 8, T], F32)
    s_mask = dscr("s_mask", [3, 128, 8, 768], BF16)
    s_wpa = nc.dram_tensor("s_wpa", [128, 4, 1024], BF16, kind="Internal").ap()
    s_wpb = nc.dram_tensor("s_wpb", [128, 4, 1024], BF16, kind="Internal").ap()
    s_wo = nc.dram_tensor("s_wo", [128, 8, 1024], BF16, kind="Internal").ap()
    s_poolw = nc.dram_tensor("s_poolw", [128, 4, 128], BF16, kind="Internal").ap()
    s_band = nc.dram_tensor("s_band", [128, 3, 1536], BF16, kind="Internal").ap()
    s_wup = nc.dram_tensor("s_wup", [11, 128, 8, 512], BF16, kind="Internal").ap()
    s_wdn = nc.dram_tensor("s_wdn", [128, NJ, 1024], BF16, kind="Internal").ap()

    def sb(name, shape, dt, stack=es):
        return stack.enter_context(nc.sbuf_tensor(name, list(shape), dt))

    banks = [es.enter_context(nc.psum_tensor("bank%d" % i, [128, 512], F32)) for i in range(8)]

    def H(k):
        return banks[k // 2][:, (k % 2) * T:(k % 2 + 1) * T]

    def PE(fn, r, w, sig=True):
        S.op("pe", fn, r, w, sig=sig)

    def ACT(fn, r, w):
        S.op("act", fn, r, w)

    def DVE(fn, r, w):
        S.op("dve", fn, r, w)

    def POOL(fn, r, w):
        S.op("pool", fn, r, w)

    def DMA(q, fn, r, w, slot):
        S.op(q, fn, r, w, slot=slot)

    ones_bf = sb("ones_bf", [128, 128], BF16)
    epsT = sb("epsT", [128, 1], F32)
    cv = sb("cv", [128, 8, 2], F32)
    scv = sb("scv", [128, 8, 2], BF16)
    bada = sb("bada", [128, 48], F32)
    gv = sb("gv", [128, 4, 8], F32)
    modT = sb("modT", [128, 48, 2], F32)
    A1 = sb("A1", [128, 8], F32)
    Actx = sb("Actx", [128, 8], F32)
    G1 = sb("G1", [128, 8], F32)
    A2 = sb("A2", [128, 8], F32)
    G2 = sb("G2", [128, 8], F32)
    bgate = sb("bgate", [128, 16], F32)
    pscale = sb("pscale", [128, 4], F32)
    kctx = sb("kctx", [128, 4, T], BF16)
    vctx = sb("vctx", [128, 2, 768], BF16)

    POOL(lambda: pool.memset(ones_bf[:], 1.0), [], ["ones"])
    POOL(lambda: pool.memset(epsT[:], EPS), [], ["eps"])
    POOL(lambda: pool.memset(vctx[:], 1.0), [], [("vctx", tl, z) for tl in range(2) for z in range(2)])
    DMA("sp", lambda: sp.dma_start(out=cv[:], in_=cvec), [], ["cv"], "cv")
    DMA("sp", lambda: sp.dma_start(out=bada[:], in_=b_ada), [], ["bada"], "bada")
    DMA("sp", lambda: sp.dma_start(out=gv[:], in_=gvec), [], ["gv"], "gv")
    DMA("sp", lambda: sp.dma_start(out=bgate[:], in_=b_gate), [], ["bgate"], "bgate")
    DMA("sp", lambda: sp.dma_start(out=pscale[:], in_=pool_scale), [], ["pscale"], "pscale")
    ACT(lambda: act.activation(out=scv[:], in_=cv[:], func=AF.Silu), ["cv"], ["scv"])
    B1 = modT[:, 0:8, 0]
    Bctx = modT[:, 0:8, 1]
    B2 = modT[:, 24:32, 0]
    SQ8 = [("sqb", c) for c in range(8)]

    def norm_front(xt, xres, sq, rs, rstd, tmp, hT, hres, Asb, Bap, hss, tag, mres=(), part="ab"):
        if "a" in part:
            ACT(lambda: act.activation(out=sq[:], in_=xt[:], func=AF.Square), [xres], SQ8)
        if "b" not in part:
            return
        for c in range(8):
            PE(lambda: pe.matmul(H(hss), ones_bf[:], sq[:, c, :], start=(c == 0), stop=(c == 7)),
               [("sqb", c), "ones"], [("H", hss)], sig=(c == 7))
        ACT(lambda: act.activation(out=rs[:], in_=H(hss), func=AF.Ln, bias=epsT[:], scale=1.0 / D),
            [("H", hss), "eps"], ["rs" + tag])
        ACT(lambda: act.activation(out=rstd[:], in_=rs[:], func=AF.Exp, scale=-0.5), ["rs" + tag], ["rstd" + tag])
        DVE(lambda: dve.tensor_tensor(out=tmp[:], in0=xt[:], in1=rstd[:].unsqueeze(1).to_broadcast([128, 8, T]),
                                      op=ALU.mult), [xres, "rstd" + tag], ["tmp" + tag])
        for c in range(8):
            POOL(lambda: pool.tensor_scalar(out=hT[:, c, :], in0=tmp[:, c, :], scalar1=Asb[:, c:c + 1],
                                            scalar2=Bap[:, c:c + 1], op0=ALU.mult, op1=ALU.add),
                 ["tmp" + tag] + list(mres), [(hres, c)])

    def post_norm_residual(ysb, sq, rs, rstd, obuf, ores, xt, xres, Gsb, hss, tag, defer=False):
        for c in range(8):
            PE(lambda: pe.matmul(H(hss), ones_bf[:], sq[:, c, :], start=(c == 0), stop=(c == 7)),
               [("sqb", c), "ones"], [("H", hss)], sig=(c == 7))
        ACT(lambda: act.activation(out=rs[:], in_=H(hss), func=AF.Ln, bias=epsT[:], scale=1.0 / D),
            [("H", hss), "eps"], ["rs" + tag])
        ACT(lambda: act.activation(out=rstd[:], in_=rs[:], func=AF.Exp, scale=-0.5), ["rs" + tag], ["rstd" + tag])
        def chunk_task(c):
            DVE(lambda: dve.scalar_tensor_tensor(out=obuf[:, c, :], in0=ysb[:, c, :], scalar=Gsb[:, c:c + 1],
                                                 in1=rstd[:], op0=ALU.mult, op1=ALU.mult),
                [("ysb", c), "rstd" + tag], [(ores, c)])
            POOL(lambda: pool.tensor_tensor(out=obuf[:, c, :], in0=obuf[:, c, :], in1=xt[:, c, :], op=ALU.add),
                 [(ores, c), xres], [(ores, c)])
        if defer:
            return [(lambda c=c: chunk_task(c)) for c in range(8)]
        for c in range(8):
            DVE(lambda: dve.scalar_tensor_tensor(out=obuf[:, c, :], in0=ysb[:, c, :], scalar=Gsb[:, c:c + 1],
                                                 in1=rstd[:], op0=ALU.mult, op1=ALU.mult),
                [("ysb", c), "rstd" + tag], [(ores, c)])
        POOL(lambda: pool.tensor_tensor(out=obuf[:], in0=obuf[:], in1=xt[:], op=ALU.add),
             [(ores, c) for c in range(8)] + [xres], [(ores, c) for c in range(8)])
        return []

    with contextlib.ExitStack() as ps:
        w1 = sb("w1", [128, 8, 8, 512], BF16, ps)
        permb = sb("permb", [128, 128], BF16, ps)
        qn = [sb("qn%d" % i, [128, 4, T], BF16, ps) for i in range(2)]
        DMA("pool", lambda: pool.dma_start(out=permb[:], in_=perm), [], ["perm"], "perm")
        for gI in [1, 2, 3, 0, 4, 5, 6, 7]:
            for kc in range(8):
                DMA("pool", lambda: pool.dma_start(out=w1[:, gI, kc, :], in_=w_in[gI][:, kc, :]),
                    [], [("w1", gI, kc)], "w1_%d" % gI)
        wst = [sb("wada%d" % i, [128, 512], F32, ps) for i in range(8)]
        wbf = [sb("wadab%d" % i, [128, 512], BF16, ps) for i in range(8)]
        mrow = [sb("mrow%d" % i, [2, 512], F32, ps) for i in range(2)]
        id2sb = sb("id2sb", [2, 2], F32, ps)
        DMA("sp", lambda: sp.dma_start(out=id2sb[:], in_=id2), [], ["id2"], "id2")
        mst = [sb("mst%d" % i, [128, 768], F32, ps) for i in range(2)]
        mbf = [sb("mbf%d" % i, [128, 768], BF16, ps) for i in range(2)]
        rot = HBRot(range(1, 8))

        def mod_load(nb):
            for kc in range(8):
                wt = wst[kc]
                DMA("sp", lambda: sp.dma_start(out=wt[:], in_=w_ada[nb, kc]), [], [("wada", kc)], "wada%d" % kc)

        def mod_compute(nb):
            ha = rot.bank()
            acc = banks[ha // 2]
            for kc in range(8):
                DVE(lambda: dve.tensor_copy(out=wbf[kc][:], in_=wst[kc][:]), [("wada", kc)], [("wadab", kc)])
            for kc in range(8):
                PE(lambda: pe.matmul(acc[0:2, :], scv[:, kc, :], wbf[kc][:], start=(kc == 0), stop=(kc == 7)),
                   [("wadab", kc), "scv"], [("H", ha)], sig=(kc == 7))
            mr = mrow[nb % 2]
            ACT(lambda: act.copy(out=mr[:], in_=acc[0:2, :]), [("H", ha)], [("mrow", nb % 2)])
            hb = rot.bank()
            tb = banks[hb // 2]
            for c in range(4):
                PE(lambda: pe.matmul(tb[:, 2 * c:2 * c + 2], mr[0:2, c * 128:(c + 1) * 128], id2sb[0:2, :],
                                     start=True, stop=True),
                   [("mrow", nb % 2), "id2"], [("H", hb)], sig=(c == 3))
            DVE(lambda: dve.tensor_tensor(out=modT[:, 4 * nb:4 * nb + 4, :],
                                          in0=tb[:, 0:8].rearrange("p (c z) -> p c z", z=2),
                                          in1=bada[:, 4 * nb:4 * nb + 4].unsqueeze(2).to_broadcast([128, 4, 2]),
                                          op=ALU.add),
                [("H", hb), "bada"], [("modT", 4 * nb + c) for c in range(4)])

        def mask_load(k):
            cls, h = divmod(k, 8)
            DMA("sp", lambda: sp.dma_start(out=mst[k % 2][:], in_=mask_add[cls, h]), [], [("mst", k % 2)], "mst%d" % (k % 2))

        def mask_item(k):
            cls, h = divmod(k, 8)
            ACT(lambda: act.activation(out=mbf[k % 2][:], in_=mst[k % 2][:], func=AF.Exp), [("mst", k % 2)], [("mbf", k % 2)])
            DMA("sp", lambda: sp.dma_start(out=s_mask[cls, :, h, :], in_=mbf[k % 2][:]), [("mbf", k % 2)],
                [("s_mask", cls, h)], "mbfst%d" % (k % 2))

        def mkA(dst, name, gi, j0, col):
            DVE(lambda: dve.scalar_tensor_tensor(out=dst[:], in0=modT[:, j0:j0 + 8, col], scalar=1.0, in1=gv[:, gi, :],
                                                 op0=ALU.add, op1=ALU.mult),
                [("modT", j) for j in range(j0, j0 + 8)] + ["gv"], [name])

        def mkG(dst, name, gi, j0):
            DVE(lambda: dve.tensor_tensor(out=dst[:], in0=modT[:, j0:j0 + 8, 0], in1=gv[:, gi, :], op=ALU.mult),
                [("modT", j) for j in range(j0, j0 + 8)] + ["gv"], [name])

        mod_load(0)
        for nb in range(4):
            mod_compute(nb)
            if nb + 1 < 4:
                mod_load(nb + 1)
        mkA(A1, "A1", 0, 8, 0)
        mkA(Actx, "Actx", 0, 8, 1)
        bg = dict(j=4, pending=[], mask=0, mpend=[])
        conv = []
        for kc in range(4):
            conv.append((s_wpa[:, kc, :], w_pa[:, kc, :]))
            conv.append((s_wpb[:, kc, :], w_pb[:, kc, :]))
        for kc in range(8):
            conv.append((s_wo[:, kc, :], w_out[:, kc, :]))
        for g in range(4):
            conv.append((s_poolw[:, g, :], pool_w[:, g, :]))
        for cls in range(3):
            for rel in range(3):
                conv.append((s_band[:, cls, rel * 512:(rel + 1) * 512], band[cls][:, rel * 512:(rel + 1) * 512]))
        for gI in range(11):
            for kc in range(8):
                conv.append((s_wup[gI][:, kc, :], w_up[gI][:, kc, :]))
        for kc in range(NJ):
            conv.append((s_wdn[:, kc, :], w_dn[:, kc, :]))
        conv_per_iter = -(-len(conv) // 17)

        def conv_some(gate=()):
            for _ in range(conv_per_iter):
                if conv:
                    o_, i_ = conv.pop(0)
                    DMA("pool", lambda: pool.dma_start(out=o_, in_=i_), list(gate), [("conv", len(conv))],
                        "conv%d" % (len(conv) % 4))

        def background(gate=()):
            conv_some(gate)
            for jj in bg["pending"]:
                mod_compute(jj)
            bg["pending"] = []
            if bg["j"] < 12:
                mod_load(bg["j"])
                bg["pending"].append(bg["j"])
                bg["j"] += 1
            for kk in bg["mpend"]:
                mask_item(kk)
            bg["mpend"] = []
            for _ in range(2):
                if bg["mask"] < 24:
                    mask_load(bg["mask"])
                    bg["mpend"].append(bg["mask"])
                    bg["mask"] += 1
        xts = [sb("xt%d" % i, [128, 8, T], F32, ps) for i in range(2)]
        rCs = [sb("rC%d" % i, [128, T], F32, ps) for i in range(2)]
        rSs = [sb("rS%d" % i, [128, T], F32, ps) for i in range(2)]
        sq = sb("sq", [128, 8, T], BF16, ps)
        rs = sb("rs", [128, T], F32, ps)
        rstd = sb("rstd", [128, T], F32, ps)
        tmp = sb("tmp", [128, 8, T], F32, ps)
        hTs = [sb("hT%d" % i, [128, 8, T], BF16, ps) for i in range(2)]
        t1s = [sb("t1_%d" % i, [128, T], F32, ps) for i in range(2)]
        t2s = [sb("t2_%d" % i, [128, T], F32, ps) for i in range(2)]
        qst = [sb("qst%d" % i, [128, 4, T], BF16, ps) for i in range(2)]
        kcst = [sb("kcst%d" % i, [128, 4, 4, 128], BF16, ps) for i in range(2)]
        vst = [sb("vst%d" % i, [128, 2, 768], BF16, ps) for i in range(2)]
        pst = [sb("pst%d" % i, [128, 2, 512], BF16, ps) for i in range(2)]
        gst = [sb("gst%d" % i, [128, 16, T], BF16, ps) for i in range(2)]
        for i in range(2):
            POOL(lambda: pool.memset(vst[i][:], 1.0), [], [(("vst", i), tl, z) for tl in range(2) for z in range(2)])
        seq = [(None, True)] + [(e, False) for e in range(NE)]

        def p1_load(pos, part="xr"):
            e, cx = seq[pos]
            b = pos % 2
            src = ctxT if cx else xT[e]
            if "x" in part:
                DMA("sp", lambda: sp.dma_start(out=xts[b][:], in_=src), [], [("xt", b)], "xt%d" % b)
            if "r" in part and not cx:
                DMA("sp", lambda: sp.dma_start(out=rCs[b][:], in_=ropeC[e]), [], [("rC", b)], "rC%d" % b)
                DMA("sp", lambda: sp.dma_start(out=rSs[b][:], in_=ropeS[e]), [], [("rS", b)], "rS%d" % b)

        def p1_front(pos, part="ab"):
            e, cx = seq[pos]
            b = pos % 2
            norm_front(xts[b], ("xt", b), sq, rs, rstd, tmp, hTs[b], ("hT", b),
                       Actx if cx else A1, Bctx if cx else B1, 0, "p1",
                       mres=["Actx" if cx else "A1"] + [("modT", j) for j in range(8)], part=part)

        def p1_proj(pos, mid):
            e, cx = seq[pos]
            b = pos % 2
            own = (not cx) and 1 <= e <= NB
            hres = ("hT", b)

            def fm_group(gI, s_, hk):
                for kc in range(8):
                    PE(lambda: pe.matmul(H(hk), w1[:, gI, kc, s_ * 128:(s_ + 1) * 128], hTs[b][:, kc, :],
                                         start=(kc == 0), stop=(kc == 7)),
                       [("w1", gI, kc), (hres, kc)], [("H", hk)], sig=(kc == 7))

            for wi, which in enumerate(["k", "q"] if own else ["k"]):
                gN = 1 if which == "k" else 0
                if cx:
                    for j0 in (0, 2):
                        ha = rot.bank()
                        fm_group(gN, j0, ha)
                        fm_group(gN, j0 + 1, ha + 1)
                        ACT(lambda: act.copy(out=kctx[:, j0:j0 + 2, :].rearrange("p a t -> p (a t)"), in_=banks[ha // 2][:]),
                            [("H", ha)], [("kctx", j0), ("kctx", j0 + 1)])
                    continue
                has = []
                for j in range(4):
                    ha = rot.bank()
                    has.append(ha)
                    fm_group(gN, j, ha)
                    ACT(lambda: act.copy(out=qn[wi][:, j, :], in_=H(ha)), [("H", ha)], [("qn", wi, j)])
                for j in range(4):
                    ha = has[j]
                    hb = ha + 1
                    PE(lambda: pe.matmul(H(hb), permb[:], qn[wi][:, j, :], start=True, stop=True),
                       ["perm", ("qn", wi, j)], [("H", hb)])
                    t1, t2 = t1s[j % 2], t2s[j % 2]
                    DVE(lambda: dve.tensor_tensor(out=t1[:], in0=H(ha), in1=rCs[b][:], op=ALU.mult),
                        [("H", ha), ("rC", b)], [("t1", j % 2)])
                    DVE(lambda: dve.tensor_tensor(out=t2[:], in0=H(hb), in1=rSs[b][:], op=ALU.mult),
                        [("H", hb), ("rS", b)], [("t2", j % 2)])
                    if which == "q":
                        DVE(lambda: dve.tensor_tensor(out=qst[b][:, j, :], in0=t1[:], in1=t2[:], op=ALU.add),
                            [("t1", j % 2), ("t2", j % 2)], [("qst", b, j)])
                    else:
                        for n in range(4):
                            a1 = t1[:].rearrange("p (r c) -> p r c", c=GW)[:, :, KCOL0[n]:KCOL0[n] + 32]
                            a2 = t2[:].rearrange("p (r c) -> p r c", c=GW)[:, :, KCOL0[n]:KCOL0[n] + 32]
                            o = kcst[b][:, j, n, :].rearrange("p (r c) -> p r c", c=32)
                            DVE(lambda: dve.tensor_tensor(out=o, in0=a1, in1=a2, op=ALU.add),
                                [("t1", j % 2), ("t2", j % 2)], [("kcst", b, j, n)])
            for which in (["v"] if cx else ["v", "p"]):
                gI = 2 if which == "v" else 3
                for tl in range(2):
                    hk = rot.bank()
                    bk = banks[hk // 2]
                    for kc in range(8):
                        PE(lambda: pe.matmul(bk[:], hTs[b][:, kc, tl * 128:(tl + 1) * 128], w1[:, gI, kc, :],
                                             start=(kc == 0), stop=(kc == 7)),
                           [("w1", gI, kc), (hres, kc)], [("H", hk)], sig=(kc == 7))
                    if which == "v":
                        dstt = vctx if cx else vst[b]
                        dres = "vctx" if cx else ("vst", b)
                        oe = dstt[:, tl, :].rearrange("p (j x) -> p j x", x=192)[:, :, 0:64]
                        ie = bk[:].rearrange("p (j x) -> p j x", x=128)[:, :, 0:64]
                        oo = dstt[:, tl, :].rearrange("p (j x) -> p j x", x=192)[:, :, 128:192]
                        io = bk[:].rearrange("p (j x) -> p j x", x=128)[:, :, 64:128]
                        ACT(lambda: act.copy(out=oe, in_=ie), [("H", hk)], [(dres, tl, 0)])
                        ACT(lambda: act.copy(out=oo, in_=io), [("H", hk)], [(dres, tl, 1)])
                    else:
                        ACT(lambda: act.copy(out=pst[b][:, tl, :], in_=bk[:]), [("H", hk)], [("pst", b, tl)])
            mid()
            if own:
                for jg in range(0, 16, 2):
                    hk = rot.bank()
                    fm_group(4 + jg // 4, jg % 4, hk)
                    fm_group(4 + (jg + 1) // 4, (jg + 1) % 4, hk + 1)
                    for z in range(2):
                        ACT(lambda: act.activation(out=gst[b][:, jg + z, :], in_=H(hk + z), func=AF.Sigmoid,
                                                   bias=bgate[:, jg + z:jg + z + 1], scale=1.0),
                            [("H", hk + z), "bgate"], [("gst", b, jg + z)])
            if cx:
                return
            i = e - 1
            DMA("sp", lambda: sp.dma_start(out=s_kc[e], in_=kcst[b][:]),
                [("kcst", b, j, n) for j in range(4) for n in range(4)], [("s_kc", e)], "kcst%d" % b)
            DMA("sp", lambda: sp.dma_start(out=s_v[e * T:(e + 1) * T, :].rearrange("(tl p) f -> p tl f", p=128),
                                           in_=vst[b][:]),
                [(("vst", b), tl, z) for tl in range(2) for z in range(2)], [("s_v", e)], "vst%d" % b)
            DMA("sp", lambda: sp.dma_start(out=s_p[e].rearrange("tl p f -> p tl f"), in_=pst[b][:]),
                [("pst", b, tl) for tl in range(2)], [("s_p", e)], "pst%d" % b)
            if own:
                DMA("sp", lambda: sp.dma_start(out=s_q[i], in_=qst[b][:]),
                    [("qst", b, j) for j in range(4)], [("s_q", i)], "qst%d" % b)
                DMA("sp", lambda: sp.dma_start(out=s_g[i], in_=gst[b][:]),
                    [("gst", b, jg) for jg in range(16)], [("s_g", i)], "gst%d" % b)

        p1_load(0)
        p1_load(1)
        p1_front(0)
        def p1_mid(pos):
            if pos + 2 < len(seq):
                p1_load(pos + 2, "r")
            background([("vctx", 1, 1)] if seq[pos][1] else [("pst", pos % 2, 1)])
            if pos + 1 < len(seq):
                p1_front(pos + 1, "b")
        for pos in range(len(seq)):
            if pos + 2 < len(seq):
                p1_load(pos + 2, "x")
            if pos + 1 < len(seq):
                p1_front(pos + 1, "a")
            p1_proj(pos, lambda: p1_mid(pos))
        background()
        assert bg["j"] == 12 and not bg["pending"] and bg["mask"] == 24 and not bg["mpend"]
        while conv:
            conv_some()
        mkG(G1, "G1", 1, 16)
        mkA(A2, "A2", 2, 32, 0)
        mkG(G2, "G2", 3, 40)
        S.barrier()

    if nphase >= 2:
        with contextlib.ExitStack() as ps:
            wpa = sb("wpa", [128, 4, 1024], BF16, ps)
            wpb = sb("wpb", [128, 4, 1024], BF16, ps)
            wo = sb("wo", [128, 8, 1024], BF16, ps)
            poolw = sb("poolw", [128, 4, 128], BF16, ps)
            bandt = sb("bandt", [128, 3, 1536], BF16, ps)
            maskt = [sb("maskt%d" % i, [128, 8, 768], BF16, ps) for i in range(2)]
            DMA("sp", lambda: sp.dma_start(out=maskt[0][:], in_=s_mask[0]), [], [("mask", 0)], "mask0")
            DMA("sp", lambda: sp.dma_start(out=maskt[1][:], in_=s_mask[1]), [], [("mask", 1)], "mask1")
            kring = [sb("kring%d" % i, [128, 4, 4, 128], BF16, ps) for i in range(3)]
            vring = [sb("vring%d" % i, [128, 4, 768], BF16, ps) for i in range(3)]
            pring = [sb("pring%d" % i, [128, 2, 512], BF16, ps) for i in range(3)]
            qb = [sb("qb%d" % i, [128, 4, T], BF16, ps) for i in range(2)]
            gb = [sb("gb%d" % i, [128, 16, T], BF16, ps) for i in range(2)]
            xtb = [sb("xtb%d" % i, [128, 8, T], F32, ps) for i in range(2)]
            pctx = [sb("pctx%d" % i, [128, 2, T], BF16, ps) for i in range(2)]
            expS = [sb("expS%d" % i, [128, 768], BF16, ps) for i in range(2)]
            PT = [sb("PT%d" % i, [128, 768], BF16, ps) for i in range(2)]
            rc = [sb("rc%d" % i, [128, T], F32, ps) for i in range(2)]
            lnd = [sb("lnd%d" % i, [128, T], F32, ps) for i in range(2)]
            attnT = sb("attnT", [128, 4, T], BF16, ps)
            pooledT = sb("pooledT", [128, 4, T], BF16, ps)
            poolT = sb("poolT", [128, 4, T], BF16, ps)
            t1s = [sb("m1_%d" % i, [128, T], F32, ps) for i in range(2)]
            t2s = [sb("m2_%d" % i, [128, T], F32, ps) for i in range(2)]
            mixT = sb("mixT", [128, 8, T], BF16, ps)
            ysb = sb("ysb", [128, 8, T], F32, ps)
            sq = sb("sq2", [128, 8, T], BF16, ps)
            rs = sb("rs2", [128, T], F32, ps)
            rstd = sb("rstd2", [128, T], F32, ps)
            x1st = [sb("x1st%d" % i, [128, 8, T], F32, ps) for i in range(2)]
            rot = HBRot([0, 1, 2, 3, 4, 5])
            SCALE = 0.125

            def ring_kv(e):
                sl = e % 3
                DMA("sp", lambda: sp.dma_start(out=kring[sl][:], in_=s_kc[e]), [], [("kr", sl)], "kr%d" % sl)
                for n in range(4):
                    for kr in range(4):
                        t0 = e * T + kr * GW + KCOL0[n]
                        DMA("sp", lambda: sp.dma_start(out=vring[sl][32 * kr:32 * kr + 32, n, :], in_=s_v[t0:t0 + 32, :]),
                            [], [("vr", sl, n, kr)], "vr%d" % sl)

            def ring_p(e):
                sl = e % 3
                DMA("sp", lambda: sp.dma_start(out=pring[sl][:], in_=s_p[e].rearrange("tl p f -> p tl f")),
                    [], [("pr", sl)], "pr%d" % sl)

            def blk_load(i):
                b = i % 2
                DMA("sp", lambda: sp.dma_start(out=qb[b][:], in_=s_q[i]), [], [("qb", b)], "qb%d" % b)
                DMA("sp", lambda: sp.dma_start(out=gb[b][:], in_=s_g[i]), [], [("gb", b)], "gb%d" % b)
                DMA("sp", lambda: sp.dma_start(out=xtb[b][:], in_=xT[i + 1]), [], [("xtb", b)], "xtb%d" % b)

            def vsl(h):
                base = 192 * (h // 2)
                return slice(base, base + 128) if h % 2 == 0 else slice(base + 64, base + 192)

            def qk(i, h):
                e = i + 1
                b = i % 2
                mi = 1 if 0 < i < NB - 1 else 0
                mt = maskt[mi]
                j, hf = h // 2, h % 2
                psl = slice(64 * hf, 64 * hf + 64)
                s_ = h % 2
                bb = 3 * s_
                for cc in range(2):
                    PE(lambda: pe.matmul(banks[bb][:, cc * T:(cc + 1) * T], kctx[psl, j, cc * 128:(cc + 1) * 128],
                                         qb[b][psl, j, :].rearrange("p (i n q) -> p n i q", n=4, q=16),
                                         start=True, stop=True),
                       [("kctx", j), ("qb", b)], [("H", 2 * bb)], sig=(cc == 1))
                ACT(lambda: act.activation(out=pctx[s_][:].rearrange("p a t -> p (a t)"), in_=banks[bb][:],
                                           func=AF.Exp, scale=SCALE),
                    [("H", 2 * bb)], [("pctx", s_)])
                for n in range(4):
                    bk = banks[bb + 1 + n // 2]
                    for jc in range(3):
                        c0 = (n % 2) * 192 + jc * 64
                        rhs = qb[b][psl, j, :].rearrange("p (r c) -> p r c", c=GW)[:, :, 16 * n:16 * n + 16]
                        sl = (e - 1 + jc) % 3
                        PE(lambda: pe.matmul(bk[:, c0:c0 + 64], kring[sl][psl, j, n, :], rhs, start=True, stop=True),
                           [("kr", sl), ("qb", b)], [("H", 2 * (bb + 1 + n // 2))], sig=(n % 2 == 1 and jc == 2))
                for half_ in range(2):
                    ACT(lambda: act.activation(out=expS[s_][:, half_ * 384:(half_ + 1) * 384],
                                               in_=banks[bb + 1 + half_][:, 0:384], func=AF.Exp, scale=SCALE),
                        [("H", 2 * (bb + 1 + half_))], [("expS", s_, half_)])
                DVE(lambda: dve.tensor_tensor(out=PT[s_][:], in0=expS[s_][:], in1=mt[:, h, :], op=ALU.mult),
                    [("expS", s_, 0), ("expS", s_, 1), ("mask", mi)], [("PT", s_)])

            def pv(i, h):
                e = i + 1
                j, hf = h // 2, h % 2
                psl = slice(64 * hf, 64 * hf + 64)
                s_ = h % 2
                hpo = 12 + 2 * s_
                po = H(hpo)
                for cc in range(2):
                    PE(lambda: pe.matmul(po, vctx[:, cc, vsl(h)], pctx[s_][:, cc, :], start=(cc == 0), stop=False),
                       [("vctx", cc, 0), ("vctx", cc, 1), ("pctx", s_)], [("H", hpo)], sig=False)
                for n in range(4):
                    for jc in range(3):
                        sl = (e - 1 + jc) % 3
                        o = po[:, n * 64:(n + 1) * 64]
                        c0 = n * 192 + jc * 64
                        last = (n == 3 and jc == 2)
                        PE(lambda: pe.matmul(o, vring[sl][:, n, vsl(h)], PT[s_][:, c0:c0 + 64], start=False, stop=last),
                           [("vr", sl, n, kr) for kr in range(4)] + [("PT", s_)], [("H", hpo)], sig=last)
                den = slice(64, 128) if hf == 0 else slice(0, 64)
                if False:
                    DVE(lambda: dve.reciprocal(out=rc[s_][psl, :], in_=po[den, :]), [("H", hpo)], [("rc", s_)])
                else:
                    ACT(lambda: act.activation(out=lnd[s_][den, :], in_=po[den, :], func=AF.Ln), [("H", hpo)], [("lnd", s_)])
                    ACT(lambda: act.activation(out=lnd[s_][den, :], in_=lnd[s_][den, :], func=AF.Exp, scale=-1.0),
                        [("lnd", s_)], [("lnd", s_)])
                    DVE(lambda: dve.tensor_copy(out=rc[s_][psl, :], in_=lnd[s_][den, :]), [("lnd", s_)], [("rc", s_)])
                DVE(lambda: dve.tensor_tensor(out=attnT[psl, j, :].rearrange("p (i n q) -> p n i q", n=4, q=16),
                                              in0=po[psl, :].rearrange("p (n i q) -> p n i q", i=4, q=16),
                                              in1=rc[s_][psl, :].rearrange("p (n i q) -> p n i q", i=4, q=16), op=ALU.mult),
                    [("H", hpo), ("rc", s_)], [("attnT", j, hf)])

            def attention(i, tail):
                qk(i, 0)
                for h in range(8):
                    if h + 1 < 8:
                        qk(i, h + 1)
                    pv(i, h)
                    if tail:
                        tail[h]()
                if tail:
                    tail[8]()

            def pooling(i):
                e = i + 1
                for g0 in (0, 2):
                    hk = rot.bank()
                    for gi in range(2):
                        g = g0 + gi
                        for tl in range(2):
                            tt = 2 * i + tl
                            bcls = 0 if tt == 0 else (2 if tt == 2 * NB - 1 else 1)
                            et = 2 * e + tl
                            for rel in range(3):
                                st = et - 1 + rel
                                sl = (st // 2) % 3
                                PE(lambda: pe.matmul(H(hk + gi)[:, tl * 128:(tl + 1) * 128],
                                                     pring[sl][:, st % 2, g * 128:(g + 1) * 128],
                                                     bandt[:, bcls, rel * 512 + g * 128:rel * 512 + (g + 1) * 128],
                                                     start=(rel == 0), stop=(rel == 2)),
                                   [("pr", sl), ("band", bcls, rel)], [("H", hk)], sig=(gi == 1 and tl == 1 and rel == 2))
                    ACT(lambda: act.copy(out=pooledT[:, g0:g0 + 2, :].rearrange("p a t -> p (a t)"), in_=banks[hk // 2][:]),
                        [("H", hk)], [("pooledT", g0), ("pooledT", g0 + 1)])
                for g0 in (0, 2):
                    hk = rot.bank()
                    for gi in range(2):
                        g = g0 + gi
                        PE(lambda: pe.matmul(H(hk + gi), poolw[:, g, :], pooledT[:, g, :], start=True, stop=True),
                           [("poolw", g), ("pooledT", g)], [("H", hk)], sig=(gi == 1))
                    for gi in range(2):
                        g = g0 + gi
                        DVE(lambda: dve.tensor_scalar(out=poolT[:, g, :], in0=H(hk + gi), scalar1=pscale[:, g:g + 1],
                                                      scalar2=None, op0=ALU.mult),
                            [("H", hk)], [("poolT", g)])

            def merge(i):
                b = i % 2
                for oc in range(8):
                    ha = rot.bank()
                    hb_ = ha + 1
                    for kc in range(4):
                        PE(lambda: pe.matmul(H(ha), wpa[:, kc, oc * 128:(oc + 1) * 128], attnT[:, kc, :],
                                             start=(kc == 0), stop=(kc == 3)),
                           [("wpa", kc), ("attnT", kc, 0), ("attnT", kc, 1)], [("H", ha)], sig=False)
                    for kc in range(4):
                        PE(lambda: pe.matmul(H(hb_), wpb[:, kc, oc * 128:(oc + 1) * 128], poolT[:, kc, :],
                                             start=(kc == 0), stop=(kc == 3)),
                           [("wpb", kc), ("poolT", kc)], [("H", hb_)], sig=(kc == 3))
                    t1, t2 = t1s[oc % 2], t2s[oc % 2]
                    DVE(lambda: dve.tensor_tensor(out=t1[:], in0=H(ha), in1=gb[b][:, oc, :], op=ALU.mult),
                        [("H", ha), ("gb", b)], [("m1", oc % 2)])
                    DVE(lambda: dve.tensor_tensor(out=t2[:], in0=H(hb_), in1=gb[b][:, 8 + oc, :], op=ALU.mult),
                        [("H", hb_), ("gb", b)], [("m2", oc % 2)])
                    POOL(lambda: pool.tensor_tensor(out=mixT[:, oc, :], in0=t1[:], in1=t2[:], op=ALU.add),
                         [("m1", oc % 2), ("m2", oc % 2)], [("mixT", oc)])

            def wout_res(i):
                b = i % 2
                for oc in range(0, 8, 2):
                    hy = rot.bank()
                    for z in range(2):
                        for kc in range(8):
                            PE(lambda: pe.matmul(H(hy + z), wo[:, kc, (oc + z) * 128:(oc + z + 1) * 128], mixT[:, kc, :],
                                                 start=(kc == 0), stop=(kc == 7)),
                               [("wo", kc), ("mixT", kc)], [("H", hy)], sig=(z == 1 and kc == 7))
                    bk = banks[hy // 2]
                    ACT(lambda: act.activation(out=sq[:, oc:oc + 2, :].rearrange("p a t -> p (a t)"), in_=bk[:], func=AF.Square),
                        [("H", hy)], [("sqb", oc), ("sqb", oc + 1)])
                    DVE(lambda: dve.tensor_copy(out=ysb[:, oc:oc + 2, :].rearrange("p a t -> p (a t)"), in_=bk[:]),
                        [("H", hy)], [("ysb", oc), ("ysb", oc + 1)])
                tasks = post_norm_residual(ysb, sq, rs, rstd, x1st[b], ("x1st", b), xtb[b], ("xtb", b), G1, 12, "p2",
                                           defer=True)
                tasks.append(lambda: DMA("sp", lambda: sp.dma_start(out=s_x1[i], in_=x1st[b][:]),
                                         [(("x1st", b), c) for c in range(8)], [("s_x1", i)], "x1st%d" % b))
                return tasks

            ring_kv(0); ring_p(0)
            ring_kv(1); ring_p(1)
            ring_kv(2); ring_p(2)
            blk_load(0)
            DMA("sp", lambda: sp.dma_start(out=wpa[:], in_=s_wpa), [], [("wpa", kc) for kc in range(4)], "wpa")
            DMA("sp", lambda: sp.dma_start(out=wpb[:], in_=s_wpb), [], [("wpb", kc) for kc in range(4)], "wpb")
            DMA("sp", lambda: sp.dma_start(out=wo[:], in_=s_wo), [], [("wo", kc) for kc in range(8)], "wo")
            DMA("sp", lambda: sp.dma_start(out=poolw[:], in_=s_poolw), [], [("poolw", g) for g in range(4)], "poolw")
            DMA("sp", lambda: sp.dma_start(out=bandt[:], in_=s_band), [],
                [("band", cls, rel) for cls in range(3) for rel in range(3)], "band")
            tail = []
            for i in range(NB):
                attention(i, tail)
                if i + 1 < NB:
                    blk_load(i + 1)
                if i == 0:
                    DMA("sp", lambda: sp.dma_start(out=maskt[0][:], in_=s_mask[2]), [], [("mask", 0)], "mask0")
                if i + 3 < NE:
                    ring_kv(i + 3)
                pooling(i)
                if i + 3 < NE:
                    ring_p(i + 3)
                merge(i)
                tail = wout_res(i)
            for t_ in tail:
                t_()
            S.barrier()

    if nphase >= 3:
        with contextlib.ExitStack() as ps:
            wup = sb("wup", [128, 11, 8, 512], BF16, ps)
            wdn = sb("wdn", [128, NJ, 1024], BF16, ps)
            x1b = [sb("x1b%d" % i, [128, 8, T], F32, ps) for i in range(2)]
            DMA("sp", lambda: sp.dma_start(out=x1b[0][:], in_=s_x1[0]), [], [("x1b", 0)], "x1b0")
            for gI in [0, 5, 1, 6, 2, 7, 3, 8, 4, 9, 10]:
                DMA("sp", lambda: sp.dma_start(out=wup[:, gI], in_=s_wup[gI]), [], [("wup", gI, kc) for kc in range(8)],
                    "wup%d" % (gI % 4))
            for q2 in range(2):
                DMA("sp", lambda: sp.dma_start(out=wdn[:, q2 * 11:(q2 + 1) * 11, :], in_=s_wdn[:, q2 * 11:(q2 + 1) * 11, :]),
                    [], [("wdn", kc) for kc in range(q2 * 11, (q2 + 1) * 11)], "wdn%d" % q2)
            sq = sb("sq3", [128, 8, T], BF16, ps)
            rs = sb("rs3", [128, T], F32, ps)
            rstd = sb("rstd3", [128, T], F32, ps)
            rsb = sb("rs3b", [128, T], F32, ps)
            rstdb = sb("rstd3b", [128, T], F32, ps)
            tmp = sb("tmp3", [128, 8, T], F32, ps)
            h2T = [sb("h2T%d" % i, [128, 8, T], BF16, ps) for i in range(2)]
            sg = [sb("sg%d" % i, [128, T], F32, ps) for i in range(2)]
            actT = sb("actT", [128, NJ, T], BF16, ps)
            ysb = sb("ysb3", [128, 8, T], F32, ps)
            ost = sb("ost", [128, 8, T], F32, ps)
            rot = HBRot(range(2, 8))
            rot.ids = [2, 3, 4, 5, 6, 7]

            def p3_load(i):
                b = i % 2
                DMA("sp", lambda: sp.dma_start(out=x1b[b][:], in_=s_x1[i]), [], [("x1b", b)], "x1b%d" % b)

            def p3_front(i, part="ab"):
                b = i % 2
                norm_front(x1b[b], ("x1b", b), sq, rs, rstd, tmp, h2T[b], ("h2T", b), A2, B2, 0, "p3", part=part)

            def ffn(i, mid):
                b = i % 2
                hres = ("h2T", b)
                for j in range(NJ):
                    if j == 5 or j == 10:
                        mid(j)
                    hg = rot.bank()
                    hu = hg + 1
                    cg, cu = j, NJ + j
                    for kc in range(8):
                        PE(lambda: pe.matmul(H(hg), wup[:, cg // 4, kc, (cg % 4) * 128:(cg % 4 + 1) * 128], h2T[b][:, kc, :],
                                             start=(kc == 0), stop=(kc == 7)),
                           [("wup", cg // 4, kc), (hres, kc)], [("H", hg)], sig=False)
                    for kc in range(8):
                        PE(lambda: pe.matmul(H(hu), wup[:, cu // 4, kc, (cu % 4) * 128:(cu % 4 + 1) * 128], h2T[b][:, kc, :],
                                             start=(kc == 0), stop=(kc == 7)),
                           [("wup", cu // 4, kc), (hres, kc)], [("H", hu)], sig=(kc == 7))
                    ACT(lambda: act.activation(out=sg[j % 2][:], in_=H(hg), func=AF.Silu), [("H", hg)], [("sg", j % 2)])
                    DVE(lambda: dve.tensor_tensor(out=actT[:, j, :], in0=H(hu), in1=sg[j % 2][:], op=ALU.mult),
                        [("H", hu), ("sg", j % 2)], [("actT", j)])
                for oc in range(0, 8, 2):
                    hy = rot.bank()
                    for z in range(2):
                        for kc in range(NJ):
                            PE(lambda: pe.matmul(H(hy + z), wdn[:, kc, (oc + z) * 128:(oc + z + 1) * 128], actT[:, kc, :],
                                                 start=(kc == 0), stop=(kc == NJ - 1)),
                               [("wdn", kc), ("actT", kc)], [("H", hy)], sig=(z == 1 and kc == NJ - 1))
                    bk = banks[hy // 2]
                    ACT(lambda: act.activation(out=sq[:, oc:oc + 2, :].rearrange("p a t -> p (a t)"), in_=bk[:], func=AF.Square),
                        [("H", hy)], [("sqb", oc), ("sqb", oc + 1)])
                    DVE(lambda: dve.tensor_copy(out=ysb[:, oc:oc + 2, :].rearrange("p a t -> p (a t)"), in_=bk[:]),
                        [("H", hy)], [("ysb", oc), ("ysb", oc + 1)])
                post_norm_residual(ysb, sq, rsb, rstdb, ost, "ost", x1b[b], ("x1b", b), G2, 2, "p3b")
                DMA("sp", lambda: sp.dma_start(out=outT[i], in_=ost[:]), [("ost", c) for c in range(8)],
                    [("outT", i)], "ost")

            p3_front(0)

            def p3_mid(i, j):
                if i + 1 < NB:
                    p3_front(i + 1, "a" if j == 5 else "b")
            for i in range(NB):
                if i + 1 < NB:
                    p3_load(i + 1)
                ffn(i, lambda j: p3_mid(i, j))

    S.final_wait("sp")
    es.close()
    return nc


_NC_CACHE = {}


def kernel(**inputs):
    maps = _prep(inputs)
    if "nc" not in _NC_CACHE:
        _NC_CACHE["nc"] = build()
    nc = _NC_CACHE["nc"]
    res = run_bass_kernel_spmd(nc, maps, core_ids=list(range(NCORE)))
    out = np.empty((4, SEQ, D), np.float32)
    for cid in range(NCORE):
        b, half = cid // 2, cid % 2
        o = res.results[cid]["outT"]
        out[b, half * 4096:(half + 1) * 4096] = o.transpose(0, 3, 2, 1).reshape(NB * T, D)
    return out
```

```python
import contextlib
import bisect
import numpy as np
import ml_dtypes
import concourse.bass as bass
import concourse.mybir as mybir
from concourse.bass_utils import run_bass_kernel_spmd

F32 = mybir.dt.float32
BF16 = mybir.dt.bfloat16
AF = mybir.ActivationFunctionType
ALU = mybir.AluOpType

D = 1024
SEQ = 8192
GW = 64
T = 256
NB = 16
NE = NB + 2
NCORE = 8
DFF = 2816
NJ = DFF // 128
EPS = 1e-6
KCOL0 = [0, 8, 24, 32]
SEM_LIM = 30000


class Sched:
    def __init__(self, nc, es):
        self.nc = nc
        self.es = es
        self.eng = {"pe": nc.tensor, "act": nc.scalar, "dve": nc.vector, "pool": nc.gpsimd, "sp": nc.sync}
        self.n = 0
        self.floor = 0
        self.info = {}
        self.lastw = {}
        self.readers = {}
        self.cnt = {e: 0 for e in self.eng}
        self.esems = {e: [] for e in self.eng}
        self.sig_i = {e: [] for e in self.eng}
        self.sig_v = {e: [] for e in self.eng}
        self.seen = {e: {} for e in self.eng}
        self.slots = {}
        self.bar_deps = {e: [] for e in self.eng}
        self.last_on = {}

    def _newsem(self, name):
        return self.es.enter_context(self.nc.semaphore(name))

    def _sig_of(self, E, d):
        k = bisect.bisect_left(self.sig_i[E], d)
        assert k < len(self.sig_i[E]), "dependency on %s op %d has no signalling successor yet" % (E, d)
        n = self.sig_v[E][k]
        return self.esems[E][n // SEM_LIM], n % SEM_LIM + 1

    def op(self, eng, fn, r=(), w=(), slot=None, sig=True):
        idx = self.n
        self.n += 1
        hb_ = [x for x in list(r) + list(w) if isinstance(x, tuple) and len(x) == 2 and x[0] == "H"]
        if hb_:
            r = [x for x in r if x not in hb_]
            w = [x for x in w if x not in hb_] + list({("B", x[1] // 2) for x in hb_})
        deps = {}
        for res in r:
            d = self.lastw.get(res)
            if d is not None and d >= self.floor:
                deps[d] = True
        for res in w:
            live = [d for d in self.readers.get(res, {}).values() if d >= self.floor]
            if live:
                for d in live:
                    deps.setdefault(d, False)
            else:
                d = self.lastw.get(res)
                if d is not None and d >= self.floor:
                    deps.setdefault(d, False)
        for d in self.bar_deps[eng]:
            deps[d] = True
        self.bar_deps[eng] = []
        engine = self.eng[eng]
        waits = {}
        for d, raw in deps.items():
            dE, dslot = self.info[d]
            if dslot is not None:
                s = self.slots[dslot]
                sem, val = s["sem"], s["cum"]
            else:
                if dE == eng and (eng == "pe" or not raw):
                    continue
                sem, val = self._sig_of(dE, d)
            key = id(sem)
            if key not in waits or waits[key][1] < val:
                waits[key] = (sem, val)
        for key, (sem, val) in waits.items():
            if self.seen[eng].get(key, 0) >= val:
                continue
            engine.wait_ge(sem, val)
            self.seen[eng][key] = val
        ins = fn()
        if slot is not None:
            s = self.slots.get(slot)
            if s is None:
                s = dict(sem=self._newsem("d%d" % len(self.slots)), cum=0)
                self.slots[slot] = s
            s["cum"] += 16
            ins.then_inc(s["sem"], 16)
        elif sig:
            n = self.cnt[eng]
            self.cnt[eng] += 1
            while len(self.esems[eng]) <= n // SEM_LIM:
                self.esems[eng].append(self._newsem("e_%s%d" % (eng, len(self.esems[eng]))))
            ins.then_inc(self.esems[eng][n // SEM_LIM], 1)
            self.sig_i[eng].append(idx)
            self.sig_v[eng].append(n)
        self.info[idx] = (eng, slot)
        for res in r:
            rd = self.readers.setdefault(res, {})
            rd[eng if slot is None else ("dma", idx)] = idx
        for res in w:
            self.lastw[res] = idx
            self.readers[res] = {}
        if slot is None:
            self.last_on[eng] = idx
        return idx

    def barrier(self):
        deps = []
        for e, idx in self.last_on.items():
            if idx >= self.floor:
                deps.append(idx)
        for i in range(self.floor, self.n):
            if self.info[i][1] is not None:
                deps.append(i)
        self.floor = self.n
        for e in self.eng:
            self.bar_deps[e] = list(deps)

    def final_wait(self, eng="sp"):
        engine = self.eng[eng]
        for s in self.slots.values():
            if s["cum"] > 0:
                engine.wait_ge(s["sem"], s["cum"])


def _chunkT(v, n):
    return np.ascontiguousarray(v.reshape(n, 128).T)


def _swap_halves(w):
    return w.reshape(w.shape[0], 8, 2, 2, 16)[:, :, :, ::-1, :].reshape(w.shape[0], 512)


def _rope_tables(tok0):
    tok = tok0 + np.arange(NE * T)
    tok = np.clip(tok, 0, SEQ - 1)
    row = (tok // GW).astype(np.float64)
    col = (tok % GW).astype(np.float64)
    p = np.arange(128)
    d = p % 64
    axis = d // 32
    hf = (d // 16) % 2
    f = d % 16
    inv = 10000.0 ** (-(f.astype(np.float64)) / 16.0)
    pos = np.where(axis[:, None] == 0, row[None, :], col[None, :])
    ang = pos * inv[:, None]
    C = np.cos(ang).astype(np.float32)
    S = (np.sin(ang) * np.where(hf == 1, 1.0, -1.0)[:, None]).astype(np.float32)
    C = np.ascontiguousarray(C.reshape(128, NE, T).transpose(1, 0, 2))
    S = np.ascontiguousarray(S.reshape(128, NE, T).transpose(1, 0, 2))
    return C, S


def _mask_tables(rpb, half):
    out = np.empty((3, 8, 128, 768), np.float32)
    n = np.arange(4)[:, None, None, None, None, None]
    jc = np.arange(3)[None, :, None, None, None, None]
    i = np.arange(4)[None, None, :, None, None, None]
    jq = np.arange(16)[None, None, None, :, None, None]
    krl = np.arange(4)[None, None, None, None, :, None]
    kcl = np.arange(32)[None, None, None, None, None, :]
    k0 = np.array(KCOL0)[n]
    for cls in range(3):
        a = [16 * half, 5, 16 * half + 15][cls]
        r = 4 * a + i
        qc = 16 * n + jq
        kr = 4 * (a - 1 + jc) + krl
        kc = k0 + kcl
        r0 = np.clip(r - 4, 0, 120)
        qc0 = np.clip(qc - 8, 0, 48)
        valid = (kr >= 0) & (kr < 128) & (kr >= r0) & (kr < r0 + 8) & (kc >= qc0) & (kc < qc0 + 16)
        ri = np.clip(kr - r + 7, 0, 14)
        ci = np.clip(kc - qc + 15, 0, 30)
        shp = np.broadcast_shapes(valid.shape, ri.shape, ci.shape)
        ri = np.broadcast_to(ri, shp)
        ci = np.broadcast_to(ci, shp)
        valid = np.broadcast_to(valid, shp)
        for h in range(8):
            b = rpb[h][ri, ci]
            b = np.where(valid, b, np.float32(-30000.0)).astype(np.float32)
            out[cls, h] = b.transpose(4, 5, 0, 1, 2, 3).reshape(128, 768)
    return out


def _band_tables(half):
    out = np.zeros((3, 128, 3, 4, 128), np.float32)
    for cls in range(3):
        gt = [32 * half, 5, 32 * half + 31][cls]
        to = gt * 128 + np.arange(128)
        for g, w in enumerate((2, 4, 8, 16)):
            lo = np.clip(to - w // 2, 0, SEQ)
            hi = np.clip(to + w - w // 2, 0, SEQ)
            cnt = (hi - lo).astype(np.float32)
            for rel in range(3):
                ts = (gt - 1 + rel) * 128 + np.arange(128)
                inwin = (ts[:, None] >= lo[None, :]) & (ts[:, None] < hi[None, :])
                wgt = np.where(inwin, 1.0 / cnt[None, :], 0.0) - (ts[:, None] == to[None, :])
                out[cls, :, rel, g, :] = wgt
    return out.reshape(3, 128, 1536)


def _prep(inp):
    f = lambda k: np.asarray(inp[k], dtype=np.float32)
    x, c, ctx, c_ctx = f("x"), f("c"), f("ctx"), f("c_ctx")
    shared = {}
    w_ada = f("w_ada")[0]
    shared["w_ada"] = np.ascontiguousarray(w_ada.reshape(8, 128, 12, 512).transpose(2, 0, 1, 3))
    shared["id2"] = np.eye(2, dtype=np.float32)
    shared["b_ada"] = _chunkT(f("b_ada")[0], 48)
    shared["gvec"] = np.ascontiguousarray(np.stack(
        [_chunkT(f(k)[0], 8) for k in ("g_pre_mix", "g_post_mix", "g_pre_ffn", "g_post_ffn")], axis=1))
    w_in = f("w_in")[0]
    q, k, v, p, g = w_in[:, :512], w_in[:, 512:1024], w_in[:, 1024:1536], w_in[:, 1536:2048], w_in[:, 2048:]
    shared["w_in"] = np.ascontiguousarray(w_in.reshape(8, 128, 8, 512).transpose(2, 1, 0, 3))
    pm = np.zeros((128, 128), np.float32)
    pm[np.arange(128) ^ 16, np.arange(128)] = 1.0
    shared["perm"] = pm
    shared["b_gate"] = _chunkT(f("b_gate")[0], 16)
    shared["pool_w"] = np.ascontiguousarray(f("pool_w")[0].transpose(1, 0, 2))
    shared["pool_scale"] = _chunkT(f("pool_scale")[0], 4)
    shared["w_pa"] = np.ascontiguousarray(f("w_proj_a")[0].reshape(4, 128, 1024).transpose(1, 0, 2))
    shared["w_pb"] = np.ascontiguousarray(f("w_proj_b")[0].reshape(4, 128, 1024).transpose(1, 0, 2))
    shared["w_out"] = np.ascontiguousarray(f("w_out")[0].reshape(8, 128, 1024).transpose(1, 0, 2))
    shared["w_up"] = np.ascontiguousarray(f("w_up")[0].reshape(8, 128, 11, 512).transpose(2, 1, 0, 3))
    shared["w_dn"] = np.ascontiguousarray(f("w_down")[0].reshape(NJ, 128, 1024).transpose(1, 0, 2))
    rpb = f("rpb")[0]
    per_half = []
    for half in range(2):
        C, S = _rope_tables(half * 4096 - T)
        per_half.append(dict(ropeC=C, ropeS=S, mask_add=_mask_tables(rpb, half), band=_band_tables(half)))
    maps = []
    for cid in range(NCORE):
        b, half = cid // 2, cid % 2
        tok0 = half * 4096 - T
        xe = np.zeros((NE * T, D), np.float32)
        lo, hi = max(tok0, 0), min(tok0 + NE * T, SEQ)
        xe[lo - tok0:hi - tok0] = x[b, lo:hi]
        m = dict(shared)
        m.update(per_half[half])
        m["xT"] = np.ascontiguousarray(xe.reshape(NE, T, 8, 128).transpose(0, 3, 2, 1))
        m["ctxT"] = np.ascontiguousarray(ctx[b].reshape(T, 8, 128).transpose(2, 1, 0))
        m["cvec"] = np.ascontiguousarray(np.stack([_chunkT(c[b], 8), _chunkT(c_ctx, 8)], axis=2))
        maps.append(m)
    return maps


class HBRot:
    def __init__(self, banks_):
        self.ids = list(banks_)
        self.k = 0

    def bank(self):
        b = self.ids[self.k % len(self.ids)]
        self.k += 1
        return 2 * b


def build(debug=False, nphase=3):
    nc = bass.Bass("TRN2", target_bir_lowering=False)
    es = contextlib.ExitStack()
    S = Sched(nc, es)
    pe, act, dve, pool, sp = nc.tensor, nc.scalar, nc.vector, nc.gpsimd, nc.sync

    def din(name, shape, dt=F32):
        return nc.dram_tensor(name, list(shape), dt, kind="ExternalInput").ap()

    def dscr(name, shape, dt):
        kind = "ExternalOutput" if debug else "Internal"
        return nc.dram_tensor(name, list(shape), dt, kind=kind).ap()

    xT = din("xT", [NE, 128, 8, T])
    ctxT = din("ctxT", [128, 8, T])
    cvec = din("cvec", [128, 8, 2])
    w_ada = din("w_ada", [12, 8, 128, 512])
    id2 = din("id2", [2, 2])
    b_ada = din("b_ada", [128, 48])
    gvec = din("gvec", [128, 4, 8])
    w_in = din("w_in", [8, 128, 8, 512])
    perm = din("perm", [128, 128])
    b_gate = din("b_gate", [128, 16])
    ropeC = din("ropeC", [NE, 128, T])
    ropeS = din("ropeS", [NE, 128, T])
    mask_add = din("mask_add", [3, 8, 128, 768])
    band = din("band", [3, 128, 1536])
    pool_w = din("pool_w", [128, 4, 128])
    pool_scale = din("pool_scale", [128, 4])
    w_pa = din("w_pa", [128, 4, 1024])
    w_pb = din("w_pb", [128, 4, 1024])
    w_out = din("w_out", [128, 8, 1024])
    w_up = din("w_up", [11, 128, 8, 512])
    w_dn = din("w_dn", [128, NJ, 1024])
    outT = nc.dram_tensor("outT", [NB, 128, 8, T], F32, kind="ExternalOutput").ap()

    s_q = dscr("s_q", [NB, 128, 4, T], BF16)
    s_kc = dscr("s_kc", [NE, 128, 4, 4, 128], BF16)
    s_v = dscr("s_v", [NE * T, 768], BF16)
    s_p = dscr("s_p", [NE, 2, 128, 512], BF16)
    s_g = dscr("s_g", [NB, 128, 16, T], BF16)
    s_x1 = dscr("s_x1", [NB, 128, 8, T], F32)
    s_mask = dscr("s_mask", [3, 128, 8, 768], BF16)
    s_wpa = nc.dram_tensor("s_wpa", [128, 4, 1024], BF16, kind="Internal").ap()
    s_wpb = nc.dram_tensor("s_wpb", [128, 4, 1024], BF16, kind="Internal").ap()
    s_wo = nc.dram_tensor("s_wo", [128, 8, 1024], BF16, kind="Internal").ap()
    s_poolw = nc.dram_tensor("s_poolw", [128, 4, 128], BF16, kind="Internal").ap()
    s_band = nc.dram_tensor("s_band", [128, 3, 1536], BF16, kind="Internal").ap()
    s_wup = nc.dram_tensor("s_wup", [11, 128, 8, 512], BF16, kind="Internal").ap()
    s_wdn = nc.dram_tensor("s_wdn", [128, NJ, 1024], BF16, kind="Internal").ap()

    def sb(name, shape, dt, stack=es):
        return stack.enter_context(nc.sbuf_tensor(name, list(shape), dt))

    banks = [es.enter_context(nc.psum_tensor("bank%d" % i, [128, 512], F32)) for i in range(8)]

    def H(k):
        return banks[k // 2][:, (k % 2) * T:(k % 2 + 1) * T]

    def PE(fn, r, w, sig=True):
        S.op("pe", fn, r, w, sig=sig)

    def ACT(fn, r, w):
        S.op("act", fn, r, w)

    def DVE(fn, r, w):
        S.op("dve", fn, r, w)

    def POOL(fn, r, w):
        S.op("pool", fn, r, w)

    def DMA(q, fn, r, w, slot):
        S.op(q, fn, r, w, slot=slot)

    ones_bf = sb("ones_bf", [128, 128], BF16)
    epsT = sb("epsT", [128, 1], F32)
    cv = sb("cv", [128, 8, 2], F32)
    scv = sb("scv", [128, 8, 2], BF16)
    bada = sb("bada", [128, 48], F32)
    gv = sb("gv", [128, 4, 8], F32)
    modT = sb("modT", [128, 48, 2], F32)
    A1 = sb("A1", [128, 8], F32)
    Actx = sb("Actx", [128, 8], F32)
    G1 = sb("G1", [128, 8], F32)
    A2 = sb("A2", [128, 8], F32)
    G2 = sb("G2", [128, 8], F32)
    bgate = sb("bgate", [128, 16], F32)
    pscale = sb("pscale", [128, 4], F32)
    kctx = sb("kctx", [128, 4, T], BF16)
    vctx = sb("vctx", [128, 2, 768], BF16)

    POOL(lambda: pool.memset(ones_bf[:], 1.0), [], ["ones"])
    POOL(lambda: pool.memset(epsT[:], EPS), [], ["eps"])
    POOL(lambda: pool.memset(vctx[:], 1.0), [], [("vctx", tl, z) for tl in range(2) for z in range(2)])
    DMA("sp", lambda: sp.dma_start(out=cv[:], in_=cvec), [], ["cv"], "cv")
    DMA("sp", lambda: sp.dma_start(out=bada[:], in_=b_ada), [], ["bada"], "bada")
    DMA("sp", lambda: sp.dma_start(out=gv[:], in_=gvec), [], ["gv"], "gv")
    DMA("sp", lambda: sp.dma_start(out=bgate[:], in_=b_gate), [], ["bgate"], "bgate")
    DMA("sp", lambda: sp.dma_start(out=pscale[:], in_=pool_scale), [], ["pscale"], "pscale")
    ACT(lambda: act.activation(out=scv[:], in_=cv[:], func=AF.Silu), ["cv"], ["scv"])
    B1 = modT[:, 0:8, 0]
    Bctx = modT[:, 0:8, 1]
    B2 = modT[:, 24:32, 0]
    SQ8 = [("sqb", c) for c in range(8)]

    def norm_front(xt, xres, sq, rs, rstd, tmp, hT, hres, Asb, Bap, hss, tag, mres=(), part="ab"):
        if "a" in part:
            ACT(lambda: act.activation(out=sq[:], in_=xt[:], func=AF.Square), [xres], SQ8)
        if "b" not in part:
            return
        for c in range(8):
            PE(lambda: pe.matmul(H(hss), ones_bf[:], sq[:, c, :], start=(c == 0), stop=(c == 7)),
               [("sqb", c), "ones"], [("H", hss)], sig=(c == 7))
        ACT(lambda: act.activation(out=rs[:], in_=H(hss), func=AF.Ln, bias=epsT[:], scale=1.0 / D),
            [("H", hss), "eps"], ["rs" + tag])
        ACT(lambda: act.activation(out=rstd[:], in_=rs[:], func=AF.Exp, scale=-0.5), ["rs" + tag], ["rstd" + tag])
        DVE(lambda: dve.tensor_tensor(out=tmp[:], in0=xt[:], in1=rstd[:].unsqueeze(1).to_broadcast([128, 8, T]),
                                      op=ALU.mult), [xres, "rstd" + tag], ["tmp" + tag])
        for c in range(8):
            POOL(lambda: pool.tensor_scalar(out=hT[:, c, :], in0=tmp[:, c, :], scalar1=Asb[:, c:c + 1],
                                            scalar2=Bap[:, c:c + 1], op0=ALU.mult, op1=ALU.add),
                 ["tmp" + tag] + list(mres), [(hres, c)])

    def post_norm_residual(ysb, sq, rs, rstd, obuf, ores, xt, xres, Gsb, hss, tag, defer=False):
        for c in range(8):
            PE(lambda: pe.matmul(H(hss), ones_bf[:], sq[:, c, :], start=(c == 0), stop=(c == 7)),
               [("sqb", c), "ones"], [("H", hss)], sig=(c == 7))
        ACT(lambda: act.activation(out=rs[:], in_=H(hss), func=AF.Ln, bias=epsT[:], scale=1.0 / D),
            [("H", hss), "eps"], ["rs" + tag])
        ACT(lambda: act.activation(out=rstd[:], in_=rs[:], func=AF.Exp, scale=-0.5), ["rs" + tag], ["rstd" + tag])
        def chunk_task(c):
            DVE(lambda: dve.scalar_tensor_tensor(out=obuf[:, c, :], in0=ysb[:, c, :], scalar=Gsb[:, c:c + 1],
                                                 in1=rstd[:], op0=ALU.mult, op1=ALU.mult),
                [("ysb", c), "rstd" + tag], [(ores, c)])
            POOL(lambda: pool.tensor_tensor(out=obuf[:, c, :], in0=obuf[:, c, :], in1=xt[:, c, :], op=ALU.add),
                 [(ores, c), xres], [(ores, c)])
        if defer:
            return [(lambda c=c: chunk_task(c)) for c in range(8)]
        for c in range(8):
            DVE(lambda: dve.scalar_tensor_tensor(out=obuf[:, c, :], in0=ysb[:, c, :], scalar=Gsb[:, c:c + 1],
                                                 in1=rstd[:], op0=ALU.mult, op1=ALU.mult),
                [("ysb", c), "rstd" + tag], [(ores, c)])
        POOL(lambda: pool.tensor_tensor(out=obuf[:], in0=obuf[:], in1=xt[:], op=ALU.add),
             [(ores, c) for c in range(8)] + [xres], [(ores, c) for c in range(8)])
        return []

    with contextlib.ExitStack() as ps:
        w1 = sb("w1", [128, 8, 8, 512], BF16, ps)
        permb = sb("permb", [128, 128], BF16, ps)
        qn = [sb("qn%d" % i, [128, 4, T], BF16, ps) for i in range(2)]
        DMA("pool", lambda: pool.dma_start(out=permb[:], in_=perm), [], ["perm"], "perm")
        for gI in [1, 2, 3, 0, 4, 5, 6, 7]:
            for kc in range(8):
                DMA("pool", lambda: pool.dma_start(out=w1[:, gI, kc, :], in_=w_in[gI][:, kc, :]),
                    [], [("w1", gI, kc)], "w1_%d" % gI)
        wst = [sb("wada%d" % i, [128, 512], F32, ps) for i in range(8)]
        wbf = [sb("wadab%d" % i, [128, 512], BF16, ps) for i in range(8)]
        mrow = [sb("mrow%d" % i, [2, 512], F32, ps) for i in range(2)]
        id2sb = sb("id2sb", [2, 2], F32, ps)
        DMA("sp", lambda: sp.dma_start(out=id2sb[:], in_=id2), [], ["id2"], "id2")
        mst = [sb("mst%d" % i, [128, 768], F32, ps) for i in range(2)]
        mbf = [sb("mbf%d" % i, [128, 768], BF16, ps) for i in range(2)]
        rot = HBRot(range(1, 8))

        def mod_load(nb):
            for kc in range(8):
                wt = wst[kc]
                DMA("sp", lambda: sp.dma_start(out=wt[:], in_=w_ada[nb, kc]), [], [("wada", kc)], "wada%d" % kc)

        def mod_compute(nb):
            ha = rot.bank()
            acc = banks[ha // 2]
            for kc in range(8):
                DVE(lambda: dve.tensor_copy(out=wbf[kc][:], in_=wst[kc][:]), [("wada", kc)], [("wadab", kc)])
            for kc in range(8):
                PE(lambda: pe.matmul(acc[0:2, :], scv[:, kc, :], wbf[kc][:], start=(kc == 0), stop=(kc == 7)),
                   [("wadab", kc), "scv"], [("H", ha)], sig=(kc == 7))
            mr = mrow[nb % 2]
            ACT(lambda: act.copy(out=mr[:], in_=acc[0:2, :]), [("H", ha)], [("mrow", nb % 2)])
            hb = rot.bank()
            tb = banks[hb // 2]
            for c in range(4):
                PE(lambda: pe.matmul(tb[:, 2 * c:2 * c + 2], mr[0:2, c * 128:(c + 1) * 128], id2sb[0:2, :],
                                     start=True, stop=True),
                   [("mrow", nb % 2), "id2"], [("H", hb)], sig=(c == 3))
            DVE(lambda: dve.tensor_tensor(out=modT[:, 4 * nb:4 * nb + 4, :],
                                          in0=tb[:, 0:8].rearrange("p (c z) -> p c z", z=2),
                                          in1=bada[:, 4 * nb:4 * nb + 4].unsqueeze(2).to_broadcast([128, 4, 2]),
                                          op=ALU.add),
                [("H", hb), "bada"], [("modT", 4 * nb + c) for c in range(4)])

        def mask_load(k):
            cls, h = divmod(k, 8)
            DMA("sp", lambda: sp.dma_start(out=mst[k % 2][:], in_=mask_add[cls, h]), [], [("mst", k % 2)], "mst%d" % (k % 2))

        def mask_item(k):
            cls, h = divmod(k, 8)
            ACT(lambda: act.activation(out=mbf[k % 2][:], in_=mst[k % 2][:], func=AF.Exp), [("mst", k % 2)], [("mbf", k % 2)])
            DMA("sp", lambda: sp.dma_start(out=s_mask[cls, :, h, :], in_=mbf[k % 2][:]), [("mbf", k % 2)],
                [("s_mask", cls, h)], "mbfst%d" % (k % 2))

        def mkA(dst, name, gi, j0, col):
            DVE(lambda: dve.scalar_tensor_tensor(out=dst[:], in0=modT[:, j0:j0 + 8, col], scalar=1.0, in1=gv[:, gi, :],
                                                 op0=ALU.add, op1=ALU.mult),
                [("modT", j) for j in range(j0, j0 + 8)] + ["gv"], [name])

        def mkG(dst, name, gi, j0):
            DVE(lambda: dve.tensor_tensor(out=dst[:], in0=modT[:, j0:j0 + 8, 0], in1=gv[:, gi, :], op=ALU.mult),
                [("modT", j) for j in range(j0, j0 + 8)] + ["gv"], [name])

        mod_load(0)
        for nb in range(4):
            mod_compute(nb)
            if nb + 1 < 4:
                mod_load(nb + 1)
        mkA(A1, "A1", 0, 8, 0)
        mkA(Actx, "Actx", 0, 8, 1)
        bg = dict(j=4, pending=[], mask=0, mpend=[])
        conv = []
        for kc in range(4):
            conv.append((s_wpa[:, kc, :], w_pa[:, kc, :]))
            conv.append((s_wpb[:, kc, :], w_pb[:, kc, :]))
        for kc in range(8):
            conv.append((s_wo[:, kc, :], w_out[:, kc, :]))
        for g in range(4):
            conv.append((s_poolw[:, g, :], pool_w[:, g, :]))
        for cls in range(3):
            for rel in range(3):
                conv.append((s_band[:, cls, rel * 512:(rel + 1) * 512], band[cls][:, rel * 512:(rel + 1) * 512]))
        for gI in range(11):
            for kc in range(8):
                conv.append((s_wup[gI][:, kc, :], w_up[gI][:, kc, :]))
        for kc in range(NJ):
            conv.append((s_wdn[:, kc, :], w_dn[:, kc, :]))
        conv_per_iter = -(-len(conv) // 17)

        def conv_some(gate=()):
            for _ in range(conv_per_iter):
                if conv:
                    o_, i_ = conv.pop(0)
                    DMA("pool", lambda: pool.dma_start(out=o_, in_=i_), list(gate), [("conv", len(conv))],
                        "conv%d" % (len(conv) % 4))

        def background(gate=()):
            conv_some(gate)
            for jj in bg["pending"]:
                mod_compute(jj)
            bg["pending"] = []
            if bg["j"] < 12:
                mod_load(bg["j"])
                bg["pending"].append(bg["j"])
                bg["j"] += 1
            for kk in bg["mpend"]:
                mask_item(kk)
            bg["mpend"] = []
            for _ in range(2):
                if bg["mask"] < 24:
                    mask_load(bg["mask"])
                    bg["mpend"].append(bg["mask"])
                    bg["mask"] += 1
        xts = [sb("xt%d" % i, [128, 8, T], F32, ps) for i in range(2)]
        rCs = [sb("rC%d" % i, [128, T], F32, ps) for i in range(2)]
        rSs = [sb("rS%d" % i, [128, T], F32, ps) for i in range(2)]
        sq = sb("sq", [128, 8, T], BF16, ps)
        rs = sb("rs", [128, T], F32, ps)
        rstd = sb("rstd", [128, T], F32, ps)
        tmp = sb("tmp", [128, 8, T], F32, ps)
        hTs = [sb("hT%d" % i, [128, 8, T], BF16, ps) for i in range(2)]
        t1s = [sb("t1_%d" % i, [128, T], F32, ps) for i in range(2)]
        t2s = [sb("t2_%d" % i, [128, T], F32, ps) for i in range(2)]
        qst = [sb("qst%d" % i, [128, 4, T], BF16, ps) for i in range(2)]
        kcst = [sb("kcst%d" % i, [128, 4, 4, 128], BF16, ps) for i in range(2)]
        vst = [sb("vst%d" % i, [128, 2, 768], BF16, ps) for i in range(2)]
        pst = [sb("pst%d" % i, [128, 2, 512], BF16, ps) for i in range(2)]
        gst = [sb("gst%d" % i, [128, 16, T], BF16, ps) for i in range(2)]
        for i in range(2):
            POOL(lambda: pool.memset(vst[i][:], 1.0), [], [(("vst", i), tl, z) for tl in range(2) for z in range(2)])
        seq = [(None, True)] + [(e, False) for e in range(NE)]

        def p1_load(pos, part="xr"):
            e, cx = seq[pos]
            b = pos % 2
            src = ctxT if cx else xT[e]
            if "x" in part:
                DMA("sp", lambda: sp.dma_start(out=xts[b][:], in_=src), [], [("xt", b)], "xt%d" % b)
            if "r" in part and not cx:
                DMA("sp", lambda: sp.dma_start(out=rCs[b][:], in_=ropeC[e]), [], [("rC", b)], "rC%d" % b)
                DMA("sp", lambda: sp.dma_start(out=rSs[b][:], in_=ropeS[e]), [], [("rS", b)], "rS%d" % b)

        def p1_front(pos, part="ab"):
            e, cx = seq[pos]
            b = pos % 2
            norm_front(xts[b], ("xt", b), sq, rs, rstd, tmp, hTs[b], ("hT", b),
                       Actx if cx else A1, Bctx if cx else B1, 0, "p1",
                       mres=["Actx" if cx else "A1"] + [("modT", j) for j in range(8)], part=part)

        def p1_proj(pos, mid):
            e, cx = seq[pos]
            b = pos % 2
            own = (not cx) and 1 <= e <= NB
            hres = ("hT", b)

            def fm_group(gI, s_, hk):
                for kc in range(8):
                    PE(lambda: pe.matmul(H(hk), w1[:, gI, kc, s_ * 128:(s_ + 1) * 128], hTs[b][:, kc, :],
                                         start=(kc == 0), stop=(kc == 7)),
                       [("w1", gI, kc), (hres, kc)], [("H", hk)], sig=(kc == 7))

            for wi, which in enumerate(["k", "q"] if own else ["k"]):
                gN = 1 if which == "k" else 0
                if cx:
                    for j0 in (0, 2):
                        ha = rot.bank()
                        fm_group(gN, j0, ha)
                        fm_group(gN, j0 + 1, ha + 1)
                        ACT(lambda: act.copy(out=kctx[:, j0:j0 + 2, :].rearrange("p a t -> p (a t)"), in_=banks[ha // 2][:]),
                            [("H", ha)], [("kctx", j0), ("kctx", j0 + 1)])
                    continue
                has = []
                for j in range(4):
                    ha = rot.bank()
                    has.append(ha)
                    fm_group(gN, j, ha)
                    ACT(lambda: act.copy(out=qn[wi][:, j, :], in_=H(ha)), [("H", ha)], [("qn", wi, j)])
                for j in range(4):
                    ha = has[j]
                    hb = ha + 1
                    PE(lambda: pe.matmul(H(hb), permb[:], qn[wi][:, j, :], start=True, stop=True),
                       ["perm", ("qn", wi, j)], [("H", hb)])
                    t1, t2 = t1s[j % 2], t2s[j % 2]
                    DVE(lambda: dve.tensor_tensor(out=t1[:], in0=H(ha), in1=rCs[b][:], op=ALU.mult),
                        [("H", ha), ("rC", b)], [("t1", j % 2)])
                    DVE(lambda: dve.tensor_tensor(out=t2[:], in0=H(hb), in1=rSs[b][:], op=ALU.mult),
                        [("H", hb), ("rS", b)], [("t2", j % 2)])
                    if which == "q":
                        DVE(lambda: dve.tensor_tensor(out=qst[b][:, j, :].rearrange("p (n i q) -> p i n q", i=4, q=16),
                                                      in0=t1[:].rearrange("p (i n q) -> p i n q", n=4, q=16),
                                                      in1=t2[:].rearrange("p (i n q) -> p i n q", n=4, q=16), op=ALU.add),
                            [("t1", j % 2), ("t2", j % 2)], [("qst", b, j)])
                    else:
                        for n in range(4):
                            a1 = t1[:].rearrange("p (r c) -> p r c", c=GW)[:, :, KCOL0[n]:KCOL0[n] + 32]
                            a2 = t2[:].rearrange("p (r c) -> p r c", c=GW)[:, :, KCOL0[n]:KCOL0[n] + 32]
                            o = kcst[b][:, j, n, :].rearrange("p (r c) -> p r c", c=32)
                            DVE(lambda: dve.tensor_tensor(out=o, in0=a1, in1=a2, op=ALU.add),
                                [("t1", j % 2), ("t2", j % 2)], [("kcst", b, j, n)])
            for which in (["v"] if cx else ["v", "p"]):
                gI = 2 if which == "v" else 3
                for tl in range(2):
                    hk = rot.bank()
                    bk = banks[hk // 2]
                    for kc in range(8):
                        PE(lambda: pe.matmul(bk[:], hTs[b][:, kc, tl * 128:(tl + 1) * 128], w1[:, gI, kc, :],
                                             start=(kc == 0), stop=(kc == 7)),
                           [("w1", gI, kc), (hres, kc)], [("H", hk)], sig=(kc == 7))
                    if which == "v":
                        dstt = vctx if cx else vst[b]
                        dres = "vctx" if cx else ("vst", b)
                        oe = dstt[:, tl, :].rearrange("p (j x) -> p j x", x=192)[:, :, 0:64]
                        ie = bk[:].rearrange("p (j x) -> p j x", x=128)[:, :, 0:64]
                        oo = dstt[:, tl, :].rearrange("p (j x) -> p j x", x=192)[:, :, 128:192]
                        io = bk[:].rearrange("p (j x) -> p j x", x=128)[:, :, 64:128]
                        ACT(lambda: act.copy(out=oe, in_=ie), [("H", hk)], [(dres, tl, 0)])
                        ACT(lambda: act.copy(out=oo, in_=io), [("H", hk)], [(dres, tl, 1)])
                    else:
                        ACT(lambda: act.copy(out=pst[b][:, tl, :], in_=bk[:]), [("H", hk)], [("pst", b, tl)])
            mid()
            if own:
                for jg in range(0, 16, 2):
                    hk = rot.bank()
                    fm_group(4 + jg // 4, jg % 4, hk)
                    fm_group(4 + (jg + 1) // 4, (jg + 1) % 4, hk + 1)
                    for z in range(2):
                        ACT(lambda: act.activation(out=gst[b][:, jg + z, :], in_=H(hk + z), func=AF.Sigmoid,
                                                   bias=bgate[:, jg + z:jg + z + 1], scale=1.0),
                            [("H", hk + z), "bgate"], [("gst", b, jg + z)])
            if cx:
                return
            i = e - 1
            DMA("sp", lambda: sp.dma_start(out=s_kc[e], in_=kcst[b][:]),
                [("kcst", b, j, n) for j in range(4) for n in range(4)], [("s_kc", e)], "kcst%d" % b)
            DMA("sp", lambda: sp.dma_start(out=s_v[e * T:(e + 1) * T, :].rearrange("(tl p) f -> p tl f", p=128),
                                           in_=vst[b][:]),
                [(("vst", b), tl, z) for tl in range(2) for z in range(2)], [("s_v", e)], "vst%d" % b)
            DMA("sp", lambda: sp.dma_start(out=s_p[e].rearrange("tl p f -> p tl f"), in_=pst[b][:]),
                [("pst", b, tl) for tl in range(2)], [("s_p", e)], "pst%d" % b)
            if own:
                DMA("sp", lambda: sp.dma_start(out=s_q[i], in_=qst[b][:]),
                    [("qst", b, j) for j in range(4)], [("s_q", i)], "qst%d" % b)
                DMA("sp", lambda: sp.dma_start(out=s_g[i], in_=gst[b][:]),
                    [("gst", b, jg) for jg in range(16)], [("s_g", i)], "gst%d" % b)

        p1_load(0)
        p1_load(1)
        p1_front(0)
        def p1_mid(pos):
            if pos + 2 < len(seq):
                p1_load(pos + 2, "r")
            background([("vctx", 1, 1)] if seq[pos][1] else [("pst", pos % 2, 1)])
            if pos + 1 < len(seq):
                p1_front(pos + 1, "b")
        for pos in range(len(seq)):
            if pos + 2 < len(seq):
                p1_load(pos + 2, "x")
            if pos + 1 < len(seq):
                p1_front(pos + 1, "a")
            p1_proj(pos, lambda: p1_mid(pos))
        background()
        assert bg["j"] == 12 and not bg["pending"] and bg["mask"] == 24 and not bg["mpend"]
        while conv:
            conv_some()
        mkG(G1, "G1", 1, 16)
        mkA(A2, "A2", 2, 32, 0)
        mkG(G2, "G2", 3, 40)
        S.barrier()

    if nphase >= 2:
        with contextlib.ExitStack() as ps:
            wpa = sb("wpa", [128, 4, 1024], BF16, ps)
            wpb = sb("wpb", [128, 4, 1024], BF16, ps)
            wo = sb("wo", [128, 8, 1024], BF16, ps)
            poolw = sb("poolw", [128, 4, 128], BF16, ps)
            bandt = sb("bandt", [128, 3, 1536], BF16, ps)
            maskt = [sb("maskt%d" % i, [128, 8, 768], BF16, ps) for i in range(2)]
            kring = [sb("kring%d" % i, [128, 4, 4, 128], BF16, ps) for i in range(3)]
            vring = [sb("vring%d" % i, [128, 4, 768], BF16, ps) for i in range(3)]
            pring = [sb("pring%d" % i, [128, 2, 512], BF16, ps) for i in range(3)]
            qb = [sb("qb%d" % i, [128, 4, T], BF16, ps) for i in range(2)]
            gb = [sb("gb%d" % i, [128, 16, T], BF16, ps) for i in range(2)]
            xtb = [sb("xtb%d" % i, [128, 8, T], F32, ps) for i in range(2)]
            pctx = [sb("pctx%d" % i, [128, 2, T], BF16, ps) for i in range(2)]
            expS = [sb("expS%d" % i, [128, 768], BF16, ps) for i in range(2)]
            PT = [sb("PT%d" % i, [128, 768], BF16, ps) for i in range(2)]
            rc = [sb("rc%d" % i, [128, T], F32, ps) for i in range(2)]
            lnd = [sb("lnd%d" % i, [128, T], F32, ps) for i in range(2)]
            attnT = sb("attnT", [128, 4, T], BF16, ps)
            pooledT = sb("pooledT", [128, 4, T], BF16, ps)
            poolT = sb("poolT", [128, 4, T], BF16, ps)
            t1s = [sb("m1_%d" % i, [128, T], F32, ps) for i in range(2)]
            t2s = [sb("m2_%d" % i, [128, T], F32, ps) for i in range(2)]
            mixT = sb("mixT", [128, 8, T], BF16, ps)
            ysb = sb("ysb", [128, 8, T], F32, ps)
            sq = sb("sq2", [128, 8, T], BF16, ps)
            rs = sb("rs2", [128, T], F32, ps)
            rstd = sb("rstd2", [128, T], F32, ps)
            x1st = [sb("x1st%d" % i, [128, 8, T], F32, ps) for i in range(2)]
            rot = HBRot([0, 1, 2, 3, 4, 5])
            SCALE = 0.125

            def ring_kv(e):
                sl = e % 3
                DMA("sp", lambda: sp.dma_start(out=kring[sl][:], in_=s_kc[e]), [], [("kr", sl)], "kr%d" % sl)
                for n in range(4):
                    for kr in range(4):
                        t0 = e * T + kr * GW + KCOL0[n]
                        DMA("sp", lambda: sp.dma_start(out=vring[sl][32 * kr:32 * kr + 32, n, :], in_=s_v[t0:t0 + 32, :]),
                            [], [("vr", sl, n, kr)], "vr%d" % sl)

            def ring_p(e):
                sl = e % 3
                DMA("sp", lambda: sp.dma_start(out=pring[sl][:], in_=s_p[e].rearrange("tl p f -> p tl f")),
                    [], [("pr", sl)], "pr%d" % sl)

            def blk_load(i, part="qgx"):
                b = i % 2
                if "q" in part:
                    DMA("sp", lambda: sp.dma_start(out=qb[b][:], in_=s_q[i]), [], [("qb", b)], "qb%d" % b)
                if "g" in part:
                    DMA("sp", lambda: sp.dma_start(out=gb[b][:], in_=s_g[i]), [], [("gb", b)], "gb%d" % b)
                if "x" in part:
                    DMA("sp", lambda: sp.dma_start(out=xtb[b][:], in_=xT[i + 1]), [], [("xtb", b)], "xtb%d" % b)

            def vsl(h):
                base = 192 * (h // 2)
                return slice(base, base + 128) if h % 2 == 0 else slice(base + 64, base + 192)

            def qk(i, h):
                e = i + 1
                b = i % 2
                mi = 1 if 0 < i < NB - 1 else 0
                mt = maskt[mi]
                j, hf = h // 2, h % 2
                psl = slice(64 * hf, 64 * hf + 64)
                s_ = h % 2
                bb = 3 * s_
                for cc in range(2):
                    PE(lambda: pe.matmul(banks[bb][:, cc * T:(cc + 1) * T], kctx[psl, j, cc * 128:(cc + 1) * 128],
                                         qb[b][psl, j, :], start=True, stop=True),
                       [("kctx", j), ("qb", b)], [("H", 2 * bb)], sig=(cc == 1))
                ACT(lambda: act.activation(out=pctx[s_][:].rearrange("p a t -> p (a t)"), in_=banks[bb][:],
                                           func=AF.Exp, scale=SCALE),
                    [("H", 2 * bb)], [("pctx", s_)])
                for n in range(4):
                    bk = banks[bb + 1 + n // 2]
                    for jc in range(3):
                        c0 = (n % 2) * 192 + jc * 64
                        rhs = qb[b][psl, j, n * 64:(n + 1) * 64]
                        sl = (e - 1 + jc) % 3
                        PE(lambda: pe.matmul(bk[:, c0:c0 + 64], kring[sl][psl, j, n, :], rhs, start=True, stop=True),
                           [("kr", sl), ("qb", b)], [("H", 2 * (bb + 1 + n // 2))], sig=(n % 2 == 1 and jc == 2))
                for half_ in range(2):
                    ACT(lambda: act.activation(out=expS[s_][:, half_ * 384:(half_ + 1) * 384],
                                               in_=banks[bb + 1 + half_][:, 0:384], func=AF.Exp, scale=SCALE),
                        [("H", 2 * (bb + 1 + half_))], [("expS", s_, half_)])
                DVE(lambda: dve.tensor_tensor(out=PT[s_][:], in0=expS[s_][:], in1=mt[:, h, :], op=ALU.mult),
                    [("expS", s_, 0), ("expS", s_, 1), ("mask", mi)], [("PT", s_)])

            def pv(i, h):
                e = i + 1
                j, hf = h // 2, h % 2
                psl = slice(64 * hf, 64 * hf + 64)
                s_ = h % 2
                hpo = 12 + 2 * s_
                po = H(hpo)
                for cc in range(2):
                    PE(lambda: pe.matmul(po, vctx[:, cc, vsl(h)], pctx[s_][:, cc, :], start=(cc == 0), stop=False),
                       [("vctx", cc, 0), ("vctx", cc, 1), ("pctx", s_)], [("H", hpo)], sig=False)
                for n in range(4):
                    for jc in range(3):
                        sl = (e - 1 + jc) % 3
                        o = po[:, n * 64:(n + 1) * 64]
                        c0 = n * 192 + jc * 64
                        last = (n == 3 and jc == 2)
                        PE(lambda: pe.matmul(o, vring[sl][:, n, vsl(h)], PT[s_][:, c0:c0 + 64], start=False, stop=last),
                           [("vr", sl, n, kr) for kr in range(4)] + [("PT", s_)], [("H", hpo)], sig=last)
                den = slice(64, 128) if hf == 0 else slice(0, 64)
                if False:
                    DVE(lambda: dve.reciprocal(out=rc[s_][psl, :], in_=po[den, :]), [("H", hpo)], [("rc", s_)])
                else:
                    ACT(lambda: act.activation(out=lnd[s_][den, :], in_=po[den, :], func=AF.Ln), [("H", hpo)], [("lnd", s_)])
                    ACT(lambda: act.activation(out=lnd[s_][den, :], in_=lnd[s_][den, :], func=AF.Exp, scale=-1.0),
                        [("lnd", s_)], [("lnd", s_)])
                    DVE(lambda: dve.tensor_copy(out=rc[s_][psl, :], in_=lnd[s_][den, :]), [("lnd", s_)], [("rc", s_)])
                DVE(lambda: dve.tensor_tensor(out=attnT[psl, j, :].rearrange("p (i n q) -> p n i q", n=4, q=16),
                                              in0=po[psl, :].rearrange("p (n i q) -> p n i q", i=4, q=16),
                                              in1=rc[s_][psl, :].rearrange("p (n i q) -> p n i q", i=4, q=16), op=ALU.mult),
                    [("H", hpo), ("rc", s_)], [("attnT", j, hf)])

            def attention(i, tail):
                qk(i, 0)
                for h in range(8):
                    if h + 1 < 8:
                        qk(i, h + 1)
                    pv(i, h)
                    if tail:
                        tail[h]()
                if tail:
                    tail[8]()

            def pooling(i):
                e = i + 1
                for g0 in (0, 2):
                    hk = rot.bank()
                    for gi in range(2):
                        g = g0 + gi
                        for tl in range(2):
                            tt = 2 * i + tl
                            bcls = 0 if tt == 0 else (2 if tt == 2 * NB - 1 else 1)
                            et = 2 * e + tl
                            for rel in range(3):
                                st = et - 1 + rel
                                sl = (st // 2) % 3
                                PE(lambda: pe.matmul(H(hk + gi)[:, tl * 128:(tl + 1) * 128],
                                                     pring[sl][:, st % 2, g * 128:(g + 1) * 128],
                                                     bandt[:, bcls, rel * 512 + g * 128:rel * 512 + (g + 1) * 128],
                                                     start=(rel == 0), stop=(rel == 2)),
                                   [("pr", sl), ("band", bcls, rel)], [("H", hk)], sig=(gi == 1 and tl == 1 and rel == 2))
                    ACT(lambda: act.copy(out=pooledT[:, g0:g0 + 2, :].rearrange("p a t -> p (a t)"), in_=banks[hk // 2][:]),
                        [("H", hk)], [("pooledT", g0), ("pooledT", g0 + 1)])
                for g0 in (0, 2):
                    hk = rot.bank()
                    for gi in range(2):
                        g = g0 + gi
                        PE(lambda: pe.matmul(H(hk + gi), poolw[:, g, :], pooledT[:, g, :], start=True, stop=True),
                           [("poolw", g), ("pooledT", g)], [("H", hk)], sig=(gi == 1))
                    for gi in range(2):
                        g = g0 + gi
                        DVE(lambda: dve.tensor_scalar(out=poolT[:, g, :], in0=H(hk + gi), scalar1=pscale[:, g:g + 1],
                                                      scalar2=None, op0=ALU.mult),
                            [("H", hk)], [("poolT", g)])

            def merge(i):
                b = i % 2
                for oc in range(8):
                    ha = rot.bank()
                    hb_ = ha + 1
                    for kc in range(4):
                        PE(lambda: pe.matmul(H(ha), wpa[:, kc, oc * 128:(oc + 1) * 128], attnT[:, kc, :],
                                             start=(kc == 0), stop=(kc == 3)),
                           [("wpa", kc), ("attnT", kc, 0), ("attnT", kc, 1)], [("H", ha)], sig=False)
                    for kc in range(4):
                        PE(lambda: pe.matmul(H(hb_), wpb[:, kc, oc * 128:(oc + 1) * 128], poolT[:, kc, :],
                                             start=(kc == 0), stop=(kc == 3)),
                           [("wpb", kc), ("poolT", kc)], [("H", hb_)], sig=(kc == 3))
                    t1, t2 = t1s[oc % 2], t2s[oc % 2]
                    DVE(lambda: dve.tensor_tensor(out=t1[:], in0=H(ha), in1=gb[b][:, oc, :], op=ALU.mult),
                        [("H", ha), ("gb", b)], [("m1", oc % 2)])
                    DVE(lambda: dve.tensor_tensor(out=t2[:], in0=H(hb_), in1=gb[b][:, 8 + oc, :], op=ALU.mult),
                        [("H", hb_), ("gb", b)], [("m2", oc % 2)])
                    POOL(lambda: pool.tensor_tensor(out=mixT[:, oc, :], in0=t1[:], in1=t2[:], op=ALU.add),
                         [("m1", oc % 2), ("m2", oc % 2)], [("mixT", oc)])

            def wout_res(i):
                b = i % 2
                for oc in range(0, 8, 2):
                    hy = rot.bank()
                    for z in range(2):
                        for kc in range(8):
                            PE(lambda: pe.matmul(H(hy + z), wo[:, kc, (oc + z) * 128:(oc + z + 1) * 128], mixT[:, kc, :],
                                                 start=(kc == 0), stop=(kc == 7)),
                               [("wo", kc), ("mixT", kc)], [("H", hy)], sig=(z == 1 and kc == 7))
                    bk = banks[hy // 2]
                    ACT(lambda: act.activation(out=sq[:, oc:oc + 2, :].rearrange("p a t -> p (a t)"), in_=bk[:], func=AF.Square),
                        [("H", hy)], [("sqb", oc), ("sqb", oc + 1)])
                    DVE(lambda: dve.tensor_copy(out=ysb[:, oc:oc + 2, :].rearrange("p a t -> p (a t)"), in_=bk[:]),
                        [("H", hy)], [("ysb", oc), ("ysb", oc + 1)])
                tasks = post_norm_residual(ysb, sq, rs, rstd, x1st[b], ("x1st", b), xtb[b], ("xtb", b), G1, 12, "p2",
                                           defer=True)
                tasks.append(lambda: DMA("sp", lambda: sp.dma_start(out=s_x1[i], in_=x1st[b][:]),
                                         [(("x1st", b), c) for c in range(8)], [("s_x1", i)], "x1st%d" % b))
                return tasks

            blk_load(0, "q")
            ring_kv(0)
            ring_kv(1)
            ring_kv(2)
            DMA("sp", lambda: sp.dma_start(out=maskt[0][:], in_=s_mask[0]), [], [("mask", 0)], "mask0")
            ring_p(0)
            ring_p(1)
            ring_p(2)
            DMA("sp", lambda: sp.dma_start(out=maskt[1][:], in_=s_mask[1]), [], [("mask", 1)], "mask1")
            blk_load(0, "gx")
            DMA("sp", lambda: sp.dma_start(out=wpa[:], in_=s_wpa), [], [("wpa", kc) for kc in range(4)], "wpa")
            DMA("sp", lambda: sp.dma_start(out=wpb[:], in_=s_wpb), [], [("wpb", kc) for kc in range(4)], "wpb")
            DMA("sp", lambda: sp.dma_start(out=wo[:], in_=s_wo), [], [("wo", kc) for kc in range(8)], "wo")
            DMA("sp", lambda: sp.dma_start(out=poolw[:], in_=s_poolw), [], [("poolw", g) for g in range(4)], "poolw")
            DMA("sp", lambda: sp.dma_start(out=bandt[:], in_=s_band), [],
                [("band", cls, rel) for cls in range(3) for rel in range(3)], "band")
            tail = []
            for i in range(NB):
                attention(i, tail)
                if i + 1 < NB:
                    blk_load(i + 1)
                if i == 0:
                    DMA("sp", lambda: sp.dma_start(out=maskt[0][:], in_=s_mask[2]), [], [("mask", 0)], "mask0")
                if i + 3 < NE:
                    ring_kv(i + 3)
                pooling(i)
                if i + 3 < NE:
                    ring_p(i + 3)
                merge(i)
                tail = wout_res(i)
            for t_ in tail:
                t_()
            S.barrier()

    if nphase >= 3:
        with contextlib.ExitStack() as ps:
            wup = sb("wup", [128, 11, 8, 512], BF16, ps)
            wdn = sb("wdn", [128, NJ, 1024], BF16, ps)
            x1b = [sb("x1b%d" % i, [128, 8, T], F32, ps) for i in range(2)]
            DMA("sp", lambda: sp.dma_start(out=x1b[0][:], in_=s_x1[0]), [], [("x1b", 0)], "x1b0")
            for gI in [0, 5, 1, 6, 2, 7, 3, 8, 4, 9, 10]:
                DMA("sp", lambda: sp.dma_start(out=wup[:, gI], in_=s_wup[gI]), [], [("wup", gI, kc) for kc in range(8)],
                    "wup%d" % (gI % 4))
            for q2 in range(2):
                DMA("sp", lambda: sp.dma_start(out=wdn[:, q2 * 11:(q2 + 1) * 11, :], in_=s_wdn[:, q2 * 11:(q2 + 1) * 11, :]),
                    [], [("wdn", kc) for kc in range(q2 * 11, (q2 + 1) * 11)], "wdn%d" % q2)
            sq = sb("sq3", [128, 8, T], BF16, ps)
            rs = sb("rs3", [128, T], F32, ps)
            rstd = sb("rstd3", [128, T], F32, ps)
            rsb = sb("rs3b", [128, T], F32, ps)
            rstdb = sb("rstd3b", [128, T], F32, ps)
            tmp = sb("tmp3", [128, 8, T], F32, ps)
            h2T = [sb("h2T%d" % i, [128, 8, T], BF16, ps) for i in range(2)]
            sg = [sb("sg%d" % i, [128, T], F32, ps) for i in range(2)]
            actT = sb("actT", [128, NJ, T], BF16, ps)
            ysb = sb("ysb3", [128, 8, T], F32, ps)
            ost = sb("ost", [128, 8, T], F32, ps)
            rot = HBRot(range(2, 8))
            rot.ids = [2, 3, 4, 5, 6, 7]

            def p3_load(i):
                b = i % 2
                DMA("sp", lambda: sp.dma_start(out=x1b[b][:], in_=s_x1[i]), [], [("x1b", b)], "x1b%d" % b)

            def p3_front(i, part="ab"):
                b = i % 2
                norm_front(x1b[b], ("x1b", b), sq, rs, rstd, tmp, h2T[b], ("h2T", b), A2, B2, 0, "p3", part=part)

            def ffn(i, mid):
                b = i % 2
                hres = ("h2T", b)
                for j in range(NJ):
                    if j == 5 or j == 10:
                        mid(j)
                    hg = rot.bank()
                    hu = hg + 1
                    cg, cu = j, NJ + j
                    for kc in range(8):
                        PE(lambda: pe.matmul(H(hg), wup[:, cg // 4, kc, (cg % 4) * 128:(cg % 4 + 1) * 128], h2T[b][:, kc, :],
                                             start=(kc == 0), stop=(kc == 7)),
                           [("wup", cg // 4, kc), (hres, kc)], [("H", hg)], sig=False)
                    for kc in range(8):
                        PE(lambda: pe.matmul(H(hu), wup[:, cu // 4, kc, (cu % 4) * 128:(cu % 4 + 1) * 128], h2T[b][:, kc, :],
                                             start=(kc == 0), stop=(kc == 7)),
                           [("wup", cu // 4, kc), (hres, kc)], [("H", hu)], sig=(kc == 7))
                    ACT(lambda: act.activation(out=sg[j % 2][:], in_=H(hg), func=AF.Silu), [("H", hg)], [("sg", j % 2)])
                    DVE(lambda: dve.tensor_tensor(out=actT[:, j, :], in0=H(hu), in1=sg[j % 2][:], op=ALU.mult),
                        [("H", hu), ("sg", j % 2)], [("actT", j)])
                for oc in range(0, 8, 2):
                    hy = rot.bank()
                    for z in range(2):
                        for kc in range(NJ):
                            PE(lambda: pe.matmul(H(hy + z), wdn[:, kc, (oc + z) * 128:(oc + z + 1) * 128], actT[:, kc, :],
                                                 start=(kc == 0), stop=(kc == NJ - 1)),
                               [("wdn", kc), ("actT", kc)], [("H", hy)], sig=(z == 1 and kc == NJ - 1))
                    bk = banks[hy // 2]
                    ACT(lambda: act.activation(out=sq[:, oc:oc + 2, :].rearrange("p a t -> p (a t)"), in_=bk[:], func=AF.Square),
                        [("H", hy)], [("sqb", oc), ("sqb", oc + 1)])
                    DVE(lambda: dve.tensor_copy(out=ysb[:, oc:oc + 2, :].rearrange("p a t -> p (a t)"), in_=bk[:]),
                        [("H", hy)], [("ysb", oc), ("ysb", oc + 1)])
                post_norm_residual(ysb, sq, rsb, rstdb, ost, "ost", x1b[b], ("x1b", b), G2, 2, "p3b")
                DMA("sp", lambda: sp.dma_start(out=outT[i], in_=ost[:]), [("ost", c) for c in range(8)],
                    [("outT", i)], "ost")

            p3_front(0)

            def p3_mid(i, j):
                if i + 1 < NB:
                    p3_front(i + 1, "a" if j == 5 else "b")
            for i in range(NB):
                if i + 1 < NB:
                    p3_load(i + 1)
                ffn(i, lambda j: p3_mid(i, j))

    S.final_wait("sp")
    es.close()
    return nc


_NC_CACHE = {}


def kernel(**inputs):
    maps = _prep(inputs)
    if "nc" not in _NC_CACHE:
        _NC_CACHE["nc"] = build()
    nc = _NC_CACHE["nc"]
    res = run_bass_kernel_spmd(nc, maps, core_ids=list(range(NCORE)))
    out = np.empty((4, SEQ, D), np.float32)
    for cid in range(NCORE):
        b, half = cid // 2, cid % 2
        o = res.results[cid]["outT"]
        out[b, half * 4096:(half + 1) * 4096] = o.transpose(0, 3, 2, 1).reshape(NB * T, D)
    return out
```
